# Optimizing a Trainium2 kernel written in Bass

```python
import jax, jax.numpy as jnp
from jax import lax
import numpy as np

D_MODEL = 1024
BATCH = 8
SEQ = 2048
DEPTH = 2

GRID_W = 64
CTX_LEN = 256
HEAD_DIM = 64
N_Q_HEADS = 8
N_KV_HEADS = 2
Q_PER_KV = N_Q_HEADS // N_KV_HEADS
ATTN_W = N_Q_HEADS * HEAD_DIM
KV_W = N_KV_HEADS * HEAD_DIM
WINDOW = 128
ATTN_BLOCK = 128
ROPE_BASE = 10000.0
POOL_WINDOWS = (2, 4, 8, 16)
POOL_GROUPS = len(POOL_WINDOWS)
POOL_W = 512
POOL_GW = POOL_W // POOL_GROUPS
RNN_W = 512
RNN_BLOCKS = 8
RNN_BW = RNN_W // RNN_BLOCKS
CONV_W = 4
LRU_C = 8.0
N_DIRS = 2
N_BRANCH = 3
D_FF = -(-8 * D_MODEL // (3 * 256)) * 256
EPS = 1e-6
NEG_INF = -1e30

Q0 = 0
K0 = Q0 + ATTN_W
V0 = K0 + KV_W
RX0 = V0 + KV_W
RY0 = RX0 + RNN_W
PU0 = RY0 + RNN_W
GL0 = PU0 + POOL_W
IN_W = GL0 + N_BRANCH * D_MODEL

kernel_name = "hybrid_gated_parallel_diffusion_block"


def rms_norm(x, g):
    xf = x.astype(jnp.float32)
    y = xf * lax.rsqrt(jnp.mean(xf * xf, axis=-1, keepdims=True) + EPS)
    return (y * g.astype(jnp.float32)).astype(x.dtype)


def modulate(h, shift, scale):
    return h * (1 + scale) + shift


def axial_rope_tables(seq_len):
    rows = seq_len // GRID_W
    row = jnp.repeat(jnp.arange(rows), GRID_W).astype(jnp.float32)
    col = jnp.tile(jnp.arange(GRID_W), rows).astype(jnp.float32)
    half = HEAD_DIM // 2
    quarter = half // 2
    inv = ROPE_BASE ** (-(jnp.arange(quarter, dtype=jnp.float32) * 2.0 / half))
    ang_r = row[:, None] * inv[None, :]
    ang_c = col[:, None] * inv[None, :]
    return jnp.cos(ang_r), jnp.sin(ang_r), jnp.cos(ang_c), jnp.sin(ang_c)


def apply_axial_rope(x, tables):
    cos_r, sin_r, cos_c, sin_c = tables
    xf = x.astype(jnp.float32)
    half = HEAD_DIM // 2
    quarter = half // 2

    def rot(u, cos, sin):
        cos = cos[None, :, None, :]
        sin = sin[None, :, None, :]
        u1, u2 = u[..., :quarter], u[..., quarter:]
        return jnp.concatenate([u1 * cos - u2 * sin, u2 * cos + u1 * sin], axis=-1)

    out = jnp.concatenate([rot(xf[..., :half], cos_r, sin_r), rot(xf[..., half:], cos_c, sin_c)], axis=-1)
    return out.astype(x.dtype)


def windowed_attention(q_lat, k_lat, v_lat, q_ctx, k_ctx, v_ctx, sink):
    B, S = q_lat.shape[0], q_lat.shape[1]
    C = k_ctx.shape[1]
    nb = S // ATTN_BLOCK
    scale = HEAD_DIM ** -0.5
    sink_hg = sink.astype(jnp.float32).reshape(N_KV_HEADS, Q_PER_KV)

    q = (q_lat * scale).reshape(B, nb, ATTN_BLOCK, N_KV_HEADS, Q_PER_KV, HEAD_DIM)
    pad = ((0, 0), (ATTN_BLOCK, ATTN_BLOCK), (0, 0), (0, 0))
    kp = jnp.pad(k_lat, pad).reshape(B, nb + 2, ATTN_BLOCK, N_KV_HEADS, HEAD_DIM)
    vp = jnp.pad(v_lat, pad).reshape(B, nb + 2, ATTN_BLOCK, N_KV_HEADS, HEAD_DIM)
    k_win = jnp.concatenate([kp[:, :nb], kp[:, 1:nb + 1], kp[:, 2:]], axis=2)
    v_win = jnp.concatenate([vp[:, :nb], vp[:, 1:nb + 1], vp[:, 2:]], axis=2)
    nk = 3 * ATTN_BLOCK

    blk = jnp.arange(nb)[:, None, None] * ATTN_BLOCK
    qpos = blk + jnp.arange(ATTN_BLOCK)[None, :, None]
    kpos = blk - ATTN_BLOCK + jnp.arange(nk)[None, None, :]
    valid = (kpos >= 0) & (kpos < S) & (jnp.abs(qpos - kpos) <= WINDOW)

    s_loc = jnp.einsum('bnqhgd,bnkhd->bnhgqk', q, k_win).astype(jnp.float32)
    s_loc = jnp.where(valid[None, :, None, None], s_loc, NEG_INF)
    s_ctx = jnp.einsum('bnqhgd,bkhd->bnhgqk', q, k_ctx).astype(jnp.float32)
    s_sink = jnp.broadcast_to(sink_hg[None, None, :, :, None, None],
                              (B, nb, N_KV_HEADS, Q_PER_KV, ATTN_BLOCK, 1))
    p = jax.nn.softmax(jnp.concatenate([s_loc, s_ctx, s_sink], axis=-1), axis=-1).astype(v_lat.dtype)
    o = (jnp.einsum('bnhgqk,bnkhd->bnqhgd', p[..., :nk], v_win)
         + jnp.einsum('bnhgqk,bkhd->bnqhgd', p[..., nk:nk + C], v_ctx))
    o_lat = o.reshape(B, S, ATTN_W)

    o_ctx = None
    if q_ctx is not None:
        qc = (q_ctx * scale).reshape(B, C, N_KV_HEADS, Q_PER_KV, HEAD_DIM)
        sc = jnp.einsum('bqhgd,bkhd->bhgqk', qc, k_ctx).astype(jnp.float32)
        sc_sink = jnp.broadcast_to(sink_hg[None, :, :, None, None], (B, N_KV_HEADS, Q_PER_KV, C, 1))
        pc = jax.nn.softmax(jnp.concatenate([sc, sc_sink], axis=-1), axis=-1).astype(v_ctx.dtype)
        o_ctx = jnp.einsum('bhgqk,bkhd->bqhgd', pc[..., :C], v_ctx).reshape(B, C, ATTN_W)
    return o_lat, o_ctx


def multiscale_pool(u, w_mix, ch_scale):
    B, L = u.shape[0], u.shape[1]
    t = jnp.arange(L)
    outs = []
    for g, w in enumerate(POOL_WINDOWS):
        ug = u[..., g * POOL_GW:(g + 1) * POOL_GW].astype(jnp.float32)
        cs = jnp.pad(jnp.cumsum(ug, axis=1), ((0, 0), (1, 0), (0, 0)))
        lo = jnp.clip(t - (w - 1) // 2, 0, L)
        hi = jnp.clip(t + w // 2 + 1, 0, L)
        mean = (cs[:, hi] - cs[:, lo]) / (hi - lo).astype(jnp.float32)[None, :, None]
        outs.append(mean - ug)
    d = jnp.concatenate(outs, axis=-1).astype(u.dtype).reshape(B, L, POOL_GROUPS, POOL_GW)
    y = jnp.einsum('blgc,gcd->blgd', d, w_mix).reshape(B, L, POOL_W)
    return y * ch_scale


def dwconv_centred(x, w, b):
    L = x.shape[1]
    xp = jnp.pad(x, ((0, 0), ((CONV_W - 1) // 2, CONV_W // 2), (0, 0)))
    y = b
    for k in range(CONV_W):
        y = y + xp[:, k:k + L] * w[k]
    return y


def lru_coeffs(x, w_a, b_a, w_x, b_x, lam):
    B, L = x.shape[0], x.shape[1]
    xb = x.reshape(B, L, RNN_BLOCKS, RNN_BW)
    r = jax.nn.sigmoid(jnp.einsum('blhi,hij->blhj', xb, w_a).reshape(B, L, RNN_W).astype(jnp.float32)
                       + b_a.astype(jnp.float32))
    i = jax.nn.sigmoid(jnp.einsum('blhi,hij->blhj', xb, w_x).reshape(B, L, RNN_W).astype(jnp.float32)
                       + b_x.astype(jnp.float32))
    log_a = LRU_C * r * jax.nn.log_sigmoid(lam.astype(jnp.float32))
    a = jnp.exp(log_a)
    mult = jnp.sqrt(-jnp.expm1(2.0 * log_a))
    return a, mult * i * x.astype(jnp.float32)


def linear_scan(a, b, h0):
    def comb(l, r):
        return (l[0] * r[0], r[0] * l[1] + r[1])
    a_cum, b_cum = lax.associative_scan(comb, (a, b), axis=1)
    return a_cum * h0[:, None, :] + b_cum


def bidir_rglru_branch(x_lat, y_lat, x_ctx, y_ctx, conv_w, conv_b, w_a, b_a, w_x, b_x, lam):
    xl = dwconv_centred(x_lat, conv_w, conv_b)
    xc = dwconv_centred(x_ctx, conv_w, conv_b)
    h0 = jnp.zeros((x_lat.shape[0], RNN_W), jnp.float32)
    h_lat = []
    h_ctx = []
    for d in range(N_DIRS):
        al, bl = lru_coeffs(xl, w_a[d], b_a[d], w_x[d], b_x[d], lam[d])
        ac, bc = lru_coeffs(xc, w_a[d], b_a[d], w_x[d], b_x[d], lam[d])
        if d == 1:
            al, bl, ac, bc = (jnp.flip(t, axis=1) for t in (al, bl, ac, bc))
        hc = linear_scan(ac, bc, h0)
        hl = linear_scan(al, bl, hc[:, -1])
        if d == 1:
            hl, hc = jnp.flip(hl, axis=1), jnp.flip(hc, axis=1)
        h_lat.append(hl)
        h_ctx.append(hc)
    out_l = ((h_lat[0] + h_lat[1]) * jax.nn.gelu(y_lat.astype(jnp.float32))).astype(x_lat.dtype)
    out_c = None
    if y_ctx is not None:
        out_c = ((h_ctx[0] + h_ctx[1]) * jax.nn.gelu(y_ctx.astype(jnp.float32))).astype(x_ctx.dtype)
    return out_l, out_c


def merge_branches(attn, pool, rnn, gate_logits, w_attn_o, w_pool_o, w_rnn_o, w_out):
    lead = gate_logits.shape[:-1]
    g = jax.nn.sigmoid(gate_logits.astype(jnp.float32)).astype(attn.dtype).reshape(*lead, N_BRANCH, D_MODEL)
    merged = (g[..., 0, :] * (attn @ w_attn_o) + g[..., 1, :] * (pool @ w_pool_o)
              + g[..., 2, :] * (rnn @ w_rnn_o))
    return merged @ w_out


def ffn_sublayer(x, shift, scale, gate, g_pre, g_post, w_gu, w_down):
    h = modulate(rms_norm(x, g_pre), shift, scale)
    gu = h @ w_gu
    f = (jax.nn.silu(gu[..., :D_FF]) * gu[..., D_FF:]) @ w_down
    return x + gate * rms_norm(f, g_post)


def setup_inputs(seed: int = 0) -> dict:
    key = jax.random.key(seed)
    ks = jax.random.split(key, 32)
    f32 = jnp.float32

    def nrm(k, shape, scale):
        return jax.random.normal(k, shape, f32) * scale

    L = DEPTH
    u = jax.random.uniform(ks[22], (L, N_DIRS, RNN_W), f32, 0.9, 0.999)
    return {
        "x": nrm(ks[0], (BATCH, SEQ, D_MODEL), 1.0),
        "c": nrm(ks[1], (BATCH, D_MODEL), 1.0),
        "ctx": nrm(ks[2], (BATCH, CTX_LEN, D_MODEL), 1.0),
        "c_ctx": nrm(ks[3], (D_MODEL,), 1.0),
        "w_ada": nrm(ks[4], (L, D_MODEL, 6 * D_MODEL), 0.5 * D_MODEL ** -0.5),
        "b_ada": nrm(ks[5], (L, 6 * D_MODEL), 0.02),
        "g_pre_mix": 1.0 + nrm(ks[6], (L, D_MODEL), 0.05),
        "g_post_mix": 1.0 + nrm(ks[7], (L, D_MODEL), 0.05),
        "g_pre_ffn": 1.0 + nrm(ks[8], (L, D_MODEL), 0.05),
        "g_post_ffn": 1.0 + nrm(ks[9], (L, D_MODEL), 0.05),
        "w_in": nrm(ks[10], (L, D_MODEL, IN_W), D_MODEL ** -0.5),
        "attn_sink": nrm(ks[11], (L, N_Q_HEADS), 0.5),
        "w_attn_o": nrm(ks[12], (L, ATTN_W, D_MODEL), ATTN_W ** -0.5),
        "pool_mix": nrm(ks[13], (L, POOL_GROUPS, POOL_GW, POOL_GW), POOL_GW ** -0.5),
        "pool_scale": 1.0 + nrm(ks[14], (L, POOL_W), 0.1),
        "w_pool_o": nrm(ks[15], (L, POOL_W, D_MODEL), POOL_W ** -0.5),
        "conv_w": nrm(ks[16], (L, CONV_W, RNN_W), CONV_W ** -0.5),
        "conv_b": nrm(ks[17], (L, RNN_W), 0.02),
        "lru_w_a": nrm(ks[18], (L, N_DIRS, RNN_BLOCKS, RNN_BW, RNN_BW), RNN_BW ** -0.5),
        "lru_b_a": nrm(ks[19], (L, N_DIRS, RNN_W), 0.02),
        "lru_w_x": nrm(ks[20], (L, N_DIRS, RNN_BLOCKS, RNN_BW, RNN_BW), RNN_BW ** -0.5),
        "lru_b_x": nrm(ks[21], (L, N_DIRS, RNN_W), 0.02),
        "lru_lambda": jnp.log(u) - jnp.log1p(-u),
        "w_rnn_o": nrm(ks[23], (L, RNN_W, D_MODEL), RNN_W ** -0.5),
        "w_out": nrm(ks[24], (L, D_MODEL, D_MODEL), D_MODEL ** -0.5),
        "w_gu": nrm(ks[25], (L, D_MODEL, 2 * D_FF), D_MODEL ** -0.5),
        "w_down": nrm(ks[26], (L, D_FF, D_MODEL), D_FF ** -0.5),
    }


def reference(x, c, ctx, c_ctx, w_ada, b_ada, g_pre_mix, g_post_mix, g_pre_ffn, g_post_ffn,
              w_in, attn_sink, w_attn_o, pool_mix, pool_scale, w_pool_o, conv_w, conv_b,
              lru_w_a, lru_b_a, lru_w_x, lru_b_x, lru_lambda, w_rnn_o, w_out, w_gu, w_down):
    B, S, D = x.shape
    C = ctx.shape[1]
    rope = axial_rope_tables(S)
    silu_c = jax.nn.silu(c)
    silu_cc = jax.nn.silu(c_ctx)[None]
    for l in range(DEPTH):
        need_ctx = l < DEPTH - 1
        mod_x = (silu_c @ w_ada[l] + b_ada[l]).reshape(B, 1, 6, D)
        mod_c = (silu_cc @ w_ada[l] + b_ada[l]).reshape(1, 1, 6, D)
        wl = w_in[l]

        h = modulate(rms_norm(x, g_pre_mix[l]), mod_x[:, :, 0], mod_x[:, :, 1])
        hc = modulate(rms_norm(ctx, g_pre_mix[l]), mod_c[:, :, 0], mod_c[:, :, 1])
        p = h @ wl
        q = apply_axial_rope(p[..., Q0:K0].reshape(B, S, N_Q_HEADS, HEAD_DIM), rope)
        k = apply_axial_rope(p[..., K0:V0].reshape(B, S, N_KV_HEADS, HEAD_DIM), rope)
        v = p[..., V0:RX0].reshape(B, S, N_KV_HEADS, HEAD_DIM)
        rx, ry, pu, gl = p[..., RX0:RY0], p[..., RY0:PU0], p[..., PU0:GL0], p[..., GL0:]

        pc_kvx = hc @ wl[:, K0:RY0]
        kc = pc_kvx[..., :KV_W].reshape(B, C, N_KV_HEADS, HEAD_DIM)
        vc = pc_kvx[..., KV_W:2 * KV_W].reshape(B, C, N_KV_HEADS, HEAD_DIM)
        rxc = pc_kvx[..., 2 * KV_W:]
        if need_ctx:
            qc = (hc @ wl[:, Q0:K0]).reshape(B, C, N_Q_HEADS, HEAD_DIM)
            pc_rest = hc @ wl[:, RY0:]
            ryc, puc, glc = pc_rest[..., :RNN_W], pc_rest[..., RNN_W:RNN_W + POOL_W], pc_rest[..., RNN_W + POOL_W:]
        else:
            qc, ryc = None, None

        attn_l, attn_c = windowed_attention(q, k, v, qc, kc, vc, attn_sink[l])
        rnn_l, rnn_c = bidir_rglru_branch(rx, ry, rxc, ryc, conv_w[l], conv_b[l], lru_w_a[l], lru_b_a[l],
                                          lru_w_x[l], lru_b_x[l], lru_lambda[l])
        pool_l = multiscale_pool(pu, pool_mix[l], pool_scale[l])
        mix_l = merge_branches(attn_l, pool_l, rnn_l, gl, w_attn_o[l], w_pool_o[l], w_rnn_o[l], w_out[l])
        x = x + mod_x[:, :, 2] * rms_norm(mix_l, g_post_mix[l])

        x = ffn_sublayer(x, mod_x[:, :, 3], mod_x[:, :, 4], mod_x[:, :, 5], g_pre_ffn[l], g_post_ffn[l], w_gu[l], w_down[l])

        if need_ctx:
            pool_c = multiscale_pool(puc, pool_mix[l], pool_scale[l])
            mix_c = merge_branches(attn_c, pool_c, rnn_c, glc, w_attn_o[l], w_pool_o[l], w_rnn_o[l], w_out[l])
            ctx = ctx + mod_c[:, :, 2] * rms_norm(mix_c, g_post_mix[l])
            ctx = ffn_sublayer(ctx, mod_c[:, :, 3], mod_c[:, :, 4], mod_c[:, :, 5], g_pre_ffn[l], g_post_ffn[l], w_gu[l], w_down[l])
    return x
```

```python
import numpy as np
from contextlib import ExitStack
import concourse.bass as bass
import concourse.mybir as mybir
from concourse.bass_utils import run_bass_kernel_spmd

F32 = mybir.dt.float32
BF16 = mybir.dt.bfloat16
AF = mybir.ActivationFunctionType
ALU = mybir.AluOpType

D = 1024
S_LAT = 2048
C_CTX = 256
T = S_LAT + C_CTX
NT = T // 128
DEPTH = 2
IN_W = 5376
D_FF = 2816
NM = D_FF // 128
EPS = 1e-6
Q0, K0, V0, RX0, RY0, PU0, GL0 = 0, 512, 640, 768, 1280, 1792, 2304


class Buf:
    __slots__ = ("name", "w", "r", "excl")

    def __init__(self, name="", fence=None, excl=False):
        self.name = name
        self.excl = excl
        self.w = None
        self.r = dict(fence) if fence else {}


def tokens_of(bufs):
    d = {}
    for b in bufs:
        if b.w is not None and d.get(b.w[0], 0) < b.w[1]:
            d[b.w[0]] = b.w[1]
        for k, v in b.r.items():
            if d.get(k, 0) < v:
                d[k] = v
    return d


class _Rec:
    def __init__(self):
        self.call = None

    def __getattr__(self, name):
        def f(*a, **kw):
            self.call = (name, a, kw)
            return self
        return f


def _record(fn):
    r = _Rec()
    fn(r)
    assert r.call is not None
    return r.call


class K:
    ENGS = ("pe", "act", "dve", "pool", "sp")
    NO_SELF_SYNC = ("pe", "sp")

    def __init__(self, nc, n_dma_sems=32):
        self.nc = nc
        self.prog = {e: [] for e in self.ENGS}
        self.cnt = {e: 0 for e in self.ENGS}
        self.seen = {e: {} for e in self.ENGS}
        self.n_dma_sems = n_dma_sems
        self.dma_cum = [0] * n_dma_sems
        self.dma_next = [0, 0]
        self.sems = {}

    def _deps(self, eng, reads, writes, relax_self=False):
        deps = {}
        raw_self = 0

        def add(k, v):
            if deps.get(k, 0) < v:
                deps[k] = v
        for b in reads:
            if b.w is not None:
                if b.w[0] == eng:
                    raw_self = max(raw_self, b.w[1])
                else:
                    add(*b.w)
            if b.excl:
                for k, v in b.r.items():
                    if k != eng:
                        add(k, v)
        for b in writes:
            if b.w is not None:
                if b.w[0] != eng or not relax_self:
                    add(*b.w)
            for k, v in b.r.items():
                if k != eng or not relax_self:
                    add(k, v)
        if raw_self:
            add(eng, raw_self)
        out = []
        seen = self.seen[eng]
        for k, v in deps.items():
            if k == eng and eng in self.NO_SELF_SYNC:
                continue
            if seen.get(k, 0) >= v:
                continue
            seen[k] = v
            out.append((k, v))
        return out

    @staticmethod
    def _mark(tok, reads, writes):
        k, v = tok
        for b in reads:
            if b.r.get(k, 0) < v:
                b.r[k] = v
        for b in writes:
            b.w = tok
            b.r = {}

    def op(self, eng, fn, reads=(), writes=()):
        waits = self._deps(eng, reads, writes, relax_self=False)
        self.cnt[eng] += 1
        tok = (eng, self.cnt[eng])
        self._mark(tok, reads, writes)
        self.prog[eng].append((waits, _record(fn), (eng, 1)))

    def dma(self, eng, fn, reads=(), writes=()):
        half = self.n_dma_sems // 2
        qi = 0 if eng == "sp" else 1
        i = qi * half + self.dma_next[qi]
        self.dma_next[qi] = (self.dma_next[qi] + 1) % half
        key = ("dma", i)
        waits = self._deps(eng, reads, writes)
        if self.dma_cum[i] > 0 and self.seen[eng].get(key, 0) < self.dma_cum[i]:
            self.seen[eng][key] = self.dma_cum[i]
            waits.append((key, self.dma_cum[i]))
        self.dma_cum[i] += 16
        tok = (key, self.dma_cum[i])
        self._mark(tok, reads, writes)
        self.prog[eng].append((waits, _record(fn), (key, 16)))

    def wait_all(self, eng, bufs):
        waits = self._deps(eng, bufs, ())
        self.prog[eng].append((waits, None, None))

    def emit(self, stack):
        nc = self.nc
        for e in self.ENGS:
            self.sems[e] = stack.enter_context(nc.semaphore("s_" + e))
        for i in range(self.n_dma_sems):
            self.sems[("dma", i)] = stack.enter_context(nc.semaphore("s_dma%d" % i))
        block = stack.enter_context(nc.Block())
        sems = self.sems

        def mk(e):
            def body(engine):
                for waits, fn, inc in self.prog[e]:
                    for k, v in waits:
                        engine.wait_ge(sems[k], v)
                    if fn is not None:
                        name, a, kw = fn
                        getattr(engine, name)(*a, **kw).then_inc(sems[inc[0]], inc[1])
            return body
        block.tensor(mk("pe"))
        block.scalar(mk("act"))
        block.vector(mk("dve"))
        block.gpsimd(mk("pool"))
        block.sync(mk("sp"))


def make_consts():
    c = {}
    c["ident"] = np.eye(128, dtype=np.float32)
    R = np.zeros((128, 128), np.float32)
    for m in range(128):
        partner = m + 16 if (m % 32) < 16 else m - 16
        R[partner, m] = 1.0
    c["rmat"] = R
    t = np.arange(S_LAT)
    row = (t // 64).astype(np.float32)
    col = (t % 64).astype(np.float32)
    inv = (np.float32(10000.0) ** (-(np.arange(16, dtype=np.float32) * np.float32(2.0) / np.float32(32)))).astype(np.float32)
    cosT = np.zeros((128, S_LAT), np.float32)
    sinT = np.zeros((128, S_LAT), np.float32)
    for p in range(128):
        j = p % 64
        f = j % 16
        ang = ((row if j < 32 else col) * inv[f]).astype(np.float32)
        sign = -1.0 if (j % 32) < 16 else 1.0
        cosT[p] = np.cos(ang)
        sinT[p] = sign * np.sin(ang)
    c["rope"] = np.stack([cosT, sinT], 1).copy()
    b = np.arange(128)[:, None]
    a = np.arange(128)[None, :]
    mask = np.ones((128, 384), np.float32)
    mask[:, 0:128] = (b <= a)
    mask[:, 256:384] = (a <= b)
    c["mask"] = mask
    c["mask2"] = np.stack([mask, mask], 1).copy()
    ones2 = np.zeros((128, 192), np.float32)
    ones2[:, 0:64] = 1.0
    ones2[:, 128:192] = 1.0
    c["ones2"] = ones2
    ec = np.zeros((128, 4, 2, 8), np.float32)
    L = 256
    for g, w in enumerate((2, 4, 8, 16)):
        for i in range(8):
            tt = i
            lo = max(tt - (w - 1) // 2, 0); hi = min(tt + w // 2 + 1, L)
            ec[:, g, 0, i] = 1.0 / (hi - lo)
            tt = L - 8 + i
            lo = max(tt - (w - 1) // 2, 0); hi = min(tt + w // 2 + 1, L)
            ec[:, g, 1, i] = 1.0 / (hi - lo)
    c["ec"] = ec.reshape(128, 64)
    return c


WEIGHT_NAMES = ["w_ada", "b_ada", "w_in", "w_attn_o", "pool_mix", "w_pool_o", "lru_w_a", "lru_w_x", "w_rnn_o", "w_out", "w_gu", "w_down"]
NV = 84


def make_vecs(inputs):
    out = np.zeros((DEPTH, 128, NV), np.float32)
    pc = lambda v: np.asarray(v, np.float32).reshape(-1, 128).T
    for l in range(DEPTH):
        t = out[l]
        for gi, gn in enumerate(["g_pre_mix", "g_post_mix", "g_pre_ffn", "g_post_ffn"]):
            t[:, gi * 8:(gi + 1) * 8] = pc(inputs[gn][l])
        for tap in range(4):
            t[:, 32 + tap:48:4] = pc(inputs["conv_w"][l, tap])
        t[:, 48:52] = pc(inputs["conv_b"][l])
        for vi, vn in enumerate(["lru_b_a", "lru_b_x", "lru_lambda"]):
            for d_ in range(2):
                t[:, 52 + vi * 8 + d_ * 4:52 + vi * 8 + d_ * 4 + 4] = pc(inputs[vn][l, d_])
        t[:, 76:80] = pc(inputs["pool_scale"][l])
        sk = np.asarray(inputs["attn_sink"][l], np.float32).reshape(4, 2)
        t[0:64, 80:84] = sk[:, 1][None, :]
        t[64:128, 80:84] = sk[:, 0][None, :]
    return out


WEIGHT_SHAPES = {
    "w_ada": [2, 1024, 6144], "b_ada": [2, 6144], "w_in": [2, 1024, 5376],
    "w_attn_o": [2, 512, 1024], "pool_mix": [2, 4, 128, 128], "w_pool_o": [2, 512, 1024],
    "lru_w_a": [2, 2, 8, 64, 64], "lru_w_x": [2, 2, 8, 64, 64], "w_rnn_o": [2, 512, 1024],
    "w_out": [2, 1024, 1024], "w_gu": [2, 1024, 5632], "w_down": [2, 2816, 1024],
}
CONST_SHAPES = {"ident": [128, 128], "rmat": [128, 128], "rope": [128, 2, 2048], "mask": [128, 384], "mask2": [128, 2, 384],
                "ones2": [128, 192], "ec": [128, 64]}


def blocks_of(lo, hi):
    out = []
    t = lo
    while t < hi:
        lim = C_CTX if t < C_CTX else hi
        n = min(512, lim - t, hi - t)
        out.append((t, n))
        t += n
    return out


def run_pipe(jobs, lag):
    n = len(jobs)
    nst = max(len(j) for j in jobs) if jobs else 0
    for step in range(n + (nst - 1) * lag):
        for st_ in range(nst):
            i = step - st_ * lag
            if 0 <= i < n and st_ < len(jobs[i]):
                jobs[i][st_]()


def build(stop=None, dumps=()):
    nc = bass.Bass("TRN2", target_bir_lowering=False)
    dram = {}
    dram["x"] = nc.dram_tensor("x", [S_LAT, D], F32, kind="ExternalInput").ap()
    dram["ctx"] = nc.dram_tensor("ctx", [C_CTX, D], F32, kind="ExternalInput").ap()
    dram["cvecT"] = nc.dram_tensor("cvecT", [128, 8, 2], F32, kind="ExternalInput").ap()
    dram["vecs"] = nc.dram_tensor("vecs", [DEPTH, 128, NV], F32, kind="ExternalInput").ap()
    for n in WEIGHT_NAMES:
        dram[n] = nc.dram_tensor(n, WEIGHT_SHAPES[n], F32, kind="ExternalInput").ap()
    for n, shp in CONST_SHAPES.items():
        dram["k_" + n] = nc.dram_tensor("k_" + n, shp, F32, kind="ExternalInput").ap()
    y = nc.dram_tensor("y", [S_LAT, D], F32, kind="ExternalOutput").ap()
    xa = nc.dram_tensor("xa", [T, D], F32, kind="Internal").ap()
    xb = nc.dram_tensor("xb", [T, D], F32, kind="Internal").ap()
    dump_out = {}

    with ExitStack() as st:
        NWORDS = 53200
        R3W = 19184
        S = st.enter_context(nc.sbuf_tensor("S", [128, NWORDS], F32))
        PS = [st.enter_context(nc.psum_tensor("ps%d" % i, [128, 1024], F32)) for i in range(4)]
        k = K(nc)

        def sview(off, shape, dt):
            n = int(np.prod(shape[1:]))
            nw = n if dt == F32 else (n + 1) // 2
            ap = S[:, off:off + nw]
            if dt == BF16:
                ap = ap.bitcast(BF16)
            if len(shape) == 3:
                ap = ap.rearrange("p (a b) -> p a b", a=shape[1])
            elif len(shape) == 4:
                ap = ap.rearrange("p (a b c) -> p a b c", a=shape[1], b=shape[2])
            return ap

        off = 0

        def take(nw):
            nonlocal off
            o = off
            off += nw
            return o
        O_R1 = take(9216)
        O_R2 = take(13824)
        O_R3 = take(R3W)
        O_WB = take(8192)
        O_CONST = take(2784)
        assert off <= NWORDS, off

        hT = sview(O_R1, [128, 8, T], BF16)
        hT_b = [Buf("hT%d" % i) for i in range(NT)]
        hT_b2 = [Buf("hTd%d" % i) for i in range(NT)]
        brT = [sview(O_R2 + 4608 * j, [128, 4, T], BF16) for j in range(3)]
        br_b = [[Buf("br%d_%d" % (j, i)) for i in range(NT)] for j in range(3)]
        WB = [sview(O_WB + 2048 * i, [128, 8, 512], BF16) for i in range(4)]
        WB_b = [Buf("wb%d" % i) for i in range(4)]
        wb_rr = [0]

        def wb_next():
            i = wb_rr[0]
            wb_rr[0] = (i + 1) % 4
            return WB[i], WB_b[i]

        co = O_CONST
        ident = sview(co, [128, 128], F32); co += 128
        rmat = sview(co, [128, 128], F32); co += 128
        ecT = sview(co, [128, 64], F32); co += 64
        scT = sview(co, [128, 8, 2], F32); co += 16
        modT_l = []; VT_l = []; gT_l = []; A0_l = []; A1_l = []; G2_l = []; G5_l = []
        for _l in range(DEPTH):
            modT_l.append(sview(co, [128, 48, 2], F32)); co += 96
            VT_l.append(sview(co, [128, NV], F32)); co += NV
            gT_l.append(VT_l[_l][:, 0:32].rearrange("p (g c) -> p g c", g=4))
            A0_l.append(sview(co, [128, 8, 2], F32)); co += 16
            A1_l.append(sview(co, [128, 8, 2], F32)); co += 16
            G2_l.append(sview(co, [128, 8, 2], F32)); co += 16
            G5_l.append(sview(co, [128, 8, 2], F32)); co += 16
        scB = sview(co, [128, 8, 2], BF16); co += 8
        c8T = sview(co, [128, 2, 2, 4], F32); co += 16
        hbT = sview(co, [128, 2, 2, 4], F32); co += 16
        NSTAT = 8
        stat = sview(co, [128, NSTAT, 4], F32); co += 4 * NSTAT
        PM = sview(co, [128, 4, 128], BF16); co += 256
        BD = sview(co, [128, 16, 128], BF16); co += 1024
        junkb = sview(co, [128, 1024], BF16); co += 512
        assert co <= O_CONST + 2784, co - O_CONST
        stat_b = [Buf("stat%d" % i) for i in range(NSTAT)]
        stat_rr = [0]
        mod_b = [Buf("mod%d" % i) for i in range(DEPTH)]
        moda_b = [Buf("moda%d" % i) for i in range(DEPTH)]
        AGa_b = [Buf("AGa%d" % i) for i in range(DEPTH)]
        gT_b = [Buf("gT%d" % i) for i in range(DEPTH)]
        AG_b = [Buf("AG%d" % i) for i in range(DEPTH)]
        cb = {n: Buf(n) for n in ["ident", "rmat", "mask", "ones2", "ec", "sc", "esk", "cw", "scB",
                                  "lb", "c8", "ps", "stat", "PM", "BD", "junk"]}

        bank = [PS[i // 2][:, (i % 2) * 512:(i % 2) * 512 + 512] for i in range(8)]
        bank_b = [Buf("bank%d" % i, excl=True) for i in range(8)]
        pair = [PS[i][:, :] for i in range(4)]

        def mm(out_ap, lhsT, rhs, start, stop, reads, writes):
            k.op("pe", lambda e: e.matmul(out_ap, lhsT=lhsT, rhs=rhs, start=start, stop=stop), reads=reads, writes=writes)

        def dump(name, ap, bufs, dt=F32):
            if name not in dumps:
                return
            shp = list(ap.shape)
            t_ = nc.dram_tensor("dbg_" + name, shp, dt, kind="ExternalOutput").ap()
            b_ = Buf("dbg_" + name)
            k.dma("sp", lambda e: e.dma_start(out=t_, in_=ap), reads=bufs, writes=[b_])
            dump_out[name] = b_

        k.dma("sp", lambda e: e.dma_start(out=ident, in_=dram["k_ident"]), writes=[cb["ident"]])
        k.dma("sp", lambda e: e.dma_start(out=rmat, in_=dram["k_rmat"]), writes=[cb["rmat"]])
        k.dma("sp", lambda e: e.dma_start(out=ecT, in_=dram["k_ec"]), writes=[cb["ec"]])
        k.dma("sp", lambda e: e.dma_start(out=scT, in_=dram["cvecT"]), writes=[cb["sc"]])
        for _l in range(DEPTH):
            k.dma("sp", lambda e, _l=_l: e.dma_start(out=VT_l[_l], in_=dram["vecs"][_l]), writes=[gT_b[_l]])
        k.op("act", lambda e: e.activation(out=scT, in_=scT, func=AF.Silu), reads=[cb["sc"]], writes=[cb["sc"]])
        k.op("act", lambda e: e.activation(out=scB, in_=scT, func=AF.Copy), reads=[cb["sc"]], writes=[cb["scB"]])

        tile_src_x = lambda i: dram["ctx"][i * 128:(i + 1) * 128, :] if i < 2 else dram["x"][(i - 2) * 128:(i - 1) * 128, :]
        xin_b = [Buf("xin%d" % i) for i in range(NT)]
        xa_b = [Buf("xa%d" % i) for i in range(NT)]
        xb_b = [Buf("xb%d" % i) for i in range(NT)]
        y_b = [Buf("y%d" % i) for i in range(NT)]

        r3_prev = []

        def r3_fence():
            f = tokens_of(r3_prev)
            r3_prev.clear()
            return f

        def r3buf(name, fence):
            b_ = Buf(name, fence)
            r3_prev.append(b_)
            return b_

        def next_stat():
            i = stat_rr[0]
            stat_rr[0] = (i + 1) % NSTAT
            return stat[:, i, :], stat_b[i]

        def rstd_of(src_ap, src_bufs):
            st_, st_b = next_stat()
            k.op("act", lambda e: e.activation(out=junkb, in_=src_ap, func=AF.Square, accum_out=st_[:, 0:1]), reads=src_bufs, writes=[cb["junk"], st_b])
            k.op("act", lambda e: e.activation(out=st_[:, 1:2], in_=st_[:, 0:1], func=AF.Sqrt, scale=1.0 / D, bias=EPS), reads=[st_b], writes=[st_b])
            k.op("dve", lambda e: e.reciprocal(out=st_[:, 2:3], in_=st_[:, 1:2]), reads=[st_b], writes=[st_b])
            return st_[:, 2:3], st_b

        def norm_stats(xt, xt_b, xs, xs_b):
            rs, rs_b = rstd_of(xt, [xt_b])
            k.op("dve", lambda e: e.tensor_scalar(out=xs, in0=xt, scalar1=rs, scalar2=None, op0=ALU.mult), reads=[xt_b, rs_b], writes=[xs_b])

        def norm_transpose(xs, xs_b, Acoef, Bcoef, coef_bufs, r, tcol, pbank):
            for half in range(2):
                pb = pbank[half]
                for kk in range(4):
                    kc = half * 4 + kk
                    k.op("pe", lambda e, kc=kc, kk=kk, pb=pb: e.transpose(bank[pb][:, kk * 128:(kk + 1) * 128], xs[:, kc * 128:(kc + 1) * 128], ident),
                         reads=[xs_b, cb["ident"]], writes=[bank_b[pb]])
                for kk in range(4):
                    kc = half * 4 + kk
                    if half == 0:
                        k.op("act", lambda e, kc=kc, kk=kk, pb=pb: e.activation(out=hT[:, kc, tcol:tcol + 128], in_=bank[pb][:, kk * 128:(kk + 1) * 128],
                                                                              func=AF.Identity, scale=Acoef[:, kc, r:r + 1], bias=Bcoef[:, kc, r:r + 1]),
                             reads=[bank_b[pb]] + coef_bufs, writes=[hT_b[tcol // 128]])
                    else:
                        k.op("dve", lambda e, kc=kc, kk=kk, pb=pb: e.tensor_scalar(out=hT[:, kc, tcol:tcol + 128], in0=bank[pb][:, kk * 128:(kk + 1) * 128],
                                                                                 scalar1=Acoef[:, kc, r:r + 1], scalar2=Bcoef[:, kc, r:r + 1], op0=ALU.mult, op1=ALU.add),
                             reads=[bank_b[pb]] + coef_bufs, writes=[hT_b2[tcol // 128]])

        mod_state = {}

        def mod_setup(o_base, fence):
            wa = [sview(o_base + 2048 * i, [128, 8, 512], BF16) for i in range(4)]
            mrow = [sview(o_base + 8192 + 512 * i, [128, 512], F32) for i in range(2)]
            mod_state.update(wa=wa, wa_b=[r3buf("wa%d" % i, fence) for i in range(4)], mrow=mrow, mrow_b=[r3buf("mrow%d" % i, fence) for i in range(2)], n=0)

        def mod_chunk(l, n_):
            ms = mod_state
            q = ms["n"]; ms["n"] += 1
            wa, wa_b, mrow, mrow_b = ms["wa"][q % 4], ms["wa_b"][q % 4], ms["mrow"][q % 2], ms["mrow_b"][q % 2]
            k.dma("pool", lambda e: e.dma_start(out=wa, in_=dram["w_ada"][l][:, n_ * 512:(n_ + 1) * 512].rearrange("(k p) n -> p k n", p=128)), writes=[wa_b])
            k.dma("sp", lambda e: e.dma_start(out=mrow[0:2, :], in_=dram["b_ada"][l:l + 1, n_ * 512:(n_ + 1) * 512].partition_broadcast(2)), writes=[mrow_b])
            for kc in range(8):
                mm(bank[6][0:2, :], scB[:, kc, :], wa[:, kc, :], kc == 0, kc == 7, [wa_b, cb["scB"]], [bank_b[6]])
            k.op("dve", lambda e: e.tensor_tensor(out=mrow[0:2, :], in0=bank[6][0:2, :], in1=mrow[0:2, :], op=ALU.add), reads=[bank_b[6], mrow_b], writes=[mrow_b])
            for cc in range(4):
                ch = l * 48 + n_ * 4 + cc
                k.op("pe", lambda e, cc=cc, ch=ch: e.transpose(bank[7][:, ch * 2:ch * 2 + 2], mrow[0:2, cc * 128:(cc + 1) * 128], ident[0:2, 0:2]),
                     reads=[mrow_b, cb["ident"]], writes=[bank_b[7]])

        def mod_finish_a(l):
            modT, gT, A0 = modT_l[l], gT_l[l], A0_l[l]
            k.op("dve", lambda e: e.tensor_copy(out=modT[:, 0:16, :], in_=bank[7][:, l * 96:l * 96 + 32].rearrange("p (c r) -> p c r", r=2)), reads=[bank_b[7]], writes=[moda_b[l]])
            for r in range(2):
                k.op("dve", lambda e, r=r: e.scalar_tensor_tensor(out=A0[:, :, r], in0=modT[:, 8:16, r], scalar=1.0, in1=gT[:, 0, :], op0=ALU.add, op1=ALU.mult), reads=[moda_b[l], gT_b[l]], writes=[AGa_b[l]])

        def mod_finish_b(l):
            modT, gT, A1, G2, G5 = modT_l[l], gT_l[l], A1_l[l], G2_l[l], G5_l[l]
            k.op("dve", lambda e: e.tensor_copy(out=modT[:, 16:48, :], in_=bank[7][:, l * 96 + 32:l * 96 + 96].rearrange("p (c r) -> p c r", r=2)), reads=[bank_b[7]], writes=[mod_b[l]])
            for r in range(2):
                k.op("dve", lambda e, r=r: e.scalar_tensor_tensor(out=A1[:, :, r], in0=modT[:, 32:40, r], scalar=1.0, in1=gT[:, 2, :], op0=ALU.add, op1=ALU.mult), reads=[mod_b[l], gT_b[l]], writes=[AG_b[l]])
                k.op("dve", lambda e, r=r: e.tensor_tensor(out=G2[:, :, r], in0=modT[:, 16:24, r], in1=gT[:, 1, :], op=ALU.mult), reads=[mod_b[l], gT_b[l]], writes=[AG_b[l]])
                k.op("dve", lambda e, r=r: e.tensor_tensor(out=G5[:, :, r], in0=modT[:, 40:48, r], in1=gT[:, 3, :], op=ALU.mult), reads=[mod_b[l], gT_b[l]], writes=[AG_b[l]])

        for l in range(DEPTH):
            need_ctx = l < DEPTH - 1
            tiles = list(range(NT)) if need_ctx else list(range(2, NT))
            blk_all = blocks_of(0, T)
            blk_out = blk_all if need_ctx else blocks_of(C_CTX, T)
            src_b = xin_b if l == 0 else xb_b
            src_ap = (lambda i: tile_src_x(i)) if l == 0 else (lambda i: xb[i * 128:(i + 1) * 128, :])
            w_in = dram["w_in"][l]

            VT = VT_l[l]
            cwT = VT[:, 32:48].rearrange("p (c t) -> p c t", c=4)
            cbT = VT[:, 48:52]
            lbT = VT[:, 52:76].rearrange("p (v d c) -> p v d c", v=3, d=2)
            psT = VT[:, 76:80]
            esk = VT[:, 80:84]
            for nm_ in ("esk", "cw", "lb", "ps"):
                cb[nm_] = Buf(nm_ + str(l))
                cb[nm_].w = gT_b[l].w
            k.op("act", lambda e: e.activation(out=esk, in_=esk, func=AF.Exp), reads=[cb["esk"]], writes=[cb["esk"]])
            k.op("act", lambda e: e.activation(out=c8T[:, 0, :, :], in_=lbT[:, 2, :, :], func=AF.Sigmoid), reads=[cb["lb"]], writes=[cb["c8"]])
            k.op("act", lambda e: e.activation(out=c8T[:, 0, :, :], in_=c8T[:, 0, :, :], func=AF.Ln), reads=[cb["c8"]], writes=[cb["c8"]])
            k.op("dve", lambda e: e.tensor_scalar(out=c8T[:, 1, :, :], in0=c8T[:, 0, :, :], scalar1=8.0, scalar2=None, op0=ALU.mult), reads=[cb["c8"]], writes=[cb["c8"]])
            k.op("dve", lambda e: e.tensor_scalar(out=c8T[:, 0, :, :], in0=c8T[:, 0, :, :], scalar1=4.0, scalar2=None, op0=ALU.mult), reads=[cb["c8"]], writes=[cb["c8"]])
            k.op("dve", lambda e: e.tensor_scalar(out=hbT, in0=lbT[:, 0:2, :, :], scalar1=0.5, scalar2=None, op0=ALU.mult), reads=[cb["lb"]], writes=[cb["c8"]])
            fence = r3_fence()
            NXT, NXS = 4, 3
            xt = [sview(O_R3 + 1024 * i, [128, 1024], F32) for i in range(NXT)]
            xt_b = [r3buf("xt%d" % i, fence) for i in range(NXT)]
            xs = [sview(O_R3 + 1024 * NXT + 1024 * i, [128, 1024], F32) for i in range(NXS)]
            xs_b = [r3buf("xs%d" % i, fence) for i in range(NXS)]
            o_mod = O_R3 + 1024 * (NXT + NXS)
            assert o_mod + 8192 + 1024 - O_R3 <= R3W
            pending = []
            if l == 0:
                mod_setup(o_mod, fence)
                for n_ in range(4):
                    mod_chunk(0, n_)
                mod_finish_a(0)
                pending = [(0, n_) for n_ in range(4, 12)] + [(l_, n_) for l_ in range(1, DEPTH) for n_ in range(12)]
            modT, A0, A1, G2, G5 = modT_l[l], A0_l[l], A1_l[l], G2_l[l], G5_l[l]
            B0 = modT[:, 0:8, :]
            B1 = modT[:, 24:32, :]
            coef_a = [AGa_b[l], moda_b[l]]
            coef_b = [AG_b[l], mod_b[l]]
            if stop == "mod":
                break
            done_l0 = [False]

            def ride_along(nmax):
                for _ in range(nmax):
                    if not pending:
                        return
                    l_, n_ = pending.pop(0)
                    mod_chunk(l_, n_)
                    if l_ == 0 and n_ == 11:
                        mod_finish_b(0)
            jobs = []
            for i in range(NT):
                def st0(i=i):
                    s_ = i % NXT
                    k.dma("sp", lambda e: e.dma_start(out=xt[s_], in_=src_ap(i)), reads=[src_b[i]], writes=[xt_b[s_]])
                    norm_stats(xt[s_], xt_b[s_], xs[i % NXS], xs_b[i % NXS])
                    ride_along(2 if i < 2 else 1)

                def st1(i=i):
                    norm_transpose(xs[i % NXS], xs_b[i % NXS], A0, B0, coef_a, 1 if i < 2 else 0, i * 128, (2 * (i % 2), 2 * (i % 2) + 1))
                jobs.append((st0, st1))
            run_pipe(jobs, 2)
            if l == 0:
                ride_along(99)
                for l_ in range(1, DEPTH):
                    mod_finish_a(l_)
                    mod_finish_b(l_)
                dump("modT", modT, [mod_b[0], moda_b[0]])
            if l == 0:
                dump("hT", hT, hT_b + hT_b2, BF16)
            if stop == "PA":
                break

            fence = r3_fence()
            o3 = O_R3
            QT = sview(o3, [128, 4, T], BF16); o3 += 4608
            KT = sview(o3, [128, 2, T], BF16); o3 += 2304
            V2 = sview(o3, [128, NT, 2, 192], BF16); o3 += 3456
            o_rope = o3
            rope = sview(o3, [128, 2, S_LAT], F32); o3 += 4096
            qraw = [sview(o3 + 512 * i, [128, 512], F32) for i in range(3)]; o3 += 1536
            t1 = [sview(o3 + 512 * i, [128, 512], F32) for i in range(2)]; o3 += 1024
            assert o3 - O_R3 <= R3W, o3 - O_R3
            QT_b = [r3buf("QT%d" % i, fence) for i in range(NT)]
            KT_b = [r3buf("KT%d" % i, fence) for i in range(NT)]
            V2_b = [r3buf("V2_%d" % i, fence) for i in range(NT)]
            rope_b = r3buf("rope", fence)
            qraw_b = [r3buf("qraw%d" % i, fence) for i in range(3)]
            t1_b = [r3buf("t1_%d" % i, fence) for i in range(2)]
            k.dma("sp", lambda e: e.dma_start(out=rope, in_=dram["k_rope"]), writes=[rope_b])
            wq, wq_b = wb_next()
            k.dma("pool", lambda e: e.dma_start(out=wq, in_=w_in[:, Q0:Q0 + 512].rearrange("(k p) n -> p k n", p=128)), writes=[wq_b])
            wkv, wkv_b = wb_next()
            for h in range(2):
                for dup in range(2):
                    k.dma("pool", lambda e, h=h, dup=dup: e.dma_start(out=wkv[:, :, h * 128 + dup * 64:h * 128 + dup * 64 + 64],
                                                                    in_=w_in[:, K0 + h * 64:K0 + (h + 1) * 64].rearrange("(k p) n -> p k n", p=128)), writes=[wkv_b])
            k.dma("pool", lambda e: e.dma_start(out=wkv[:, :, 256:384], in_=w_in[:, V0:V0 + 128].rearrange("(k p) n -> p k n", p=128)), writes=[wkv_b])
            k.op("dve", lambda e: e.memset(V2, 1.0), writes=V2_b)
            rr = 0
            jobs = []
            jn = 0
            for which, nch in (("q", 4), ("k", 2)):
                for c_ in range(nch):
                    blks = blk_out if which == "q" else blk_all
                    for (t0, n) in blks:
                        wt_, wtb_ = (wq, wq_b) if which == "q" else (wkv, wkv_b)
                        dst, dst_b = (QT, QT_b) if which == "q" else (KT, KT_b)
                        tl = list(range(t0 // 128, (t0 + n) // 128))
                        pb = jn % 3
                        qi = jn % 3
                        jn += 1

                        def st0(c_=c_, t0=t0, n=n, wt_=wt_, wtb_=wtb_, dst=dst, dst_b=dst_b, tl=tl, pb=pb, qi=qi):
                            for kc in range(8):
                                mm(bank[pb][:, 0:n], wt_[:, kc, c_ * 128:(c_ + 1) * 128], hT[:, kc, t0:t0 + n], kc == 0, kc == 7,
                                   [wtb_] + [hT_b[i] for i in tl] + [hT_b2[i] for i in tl], [bank_b[pb]])
                            if t0 < C_CTX:
                                k.op("act", lambda e: e.activation(out=dst[:, c_, t0:t0 + n], in_=bank[pb][:, 0:n], func=AF.Copy),
                                     reads=[bank_b[pb]], writes=[dst_b[i] for i in tl])
                            else:
                                k.op("act", lambda e: e.activation(out=qraw[qi][:, 0:n], in_=bank[pb][:, 0:n], func=AF.Copy),
                                     reads=[bank_b[pb]], writes=[qraw_b[qi]])

                        def st1(c_=c_, t0=t0, n=n, dst=dst, dst_b=dst_b, tl=tl, pb=pb, qi=qi):
                            if t0 < C_CTX:
                                return
                            s0 = t0 - C_CTX
                            pb2 = 3 + pb
                            ti = qi % 2
                            mm(bank[pb2][:, 0:n], rmat, qraw[qi][:, 0:n], True, True, [cb["rmat"], qraw_b[qi]], [bank_b[pb2]])
                            k.op("pool", lambda e: e.tensor_tensor(out=t1[ti][:, 0:n], in0=qraw[qi][:, 0:n], in1=rope[:, 0, s0:s0 + n], op=ALU.mult),
                                 reads=[qraw_b[qi], rope_b], writes=[t1_b[ti]])
                            k.op("dve", lambda e: e.tensor_tensor(out=qraw[qi][:, 0:n], in0=bank[pb2][:, 0:n], in1=rope[:, 1, s0:s0 + n], op=ALU.mult),
                                 reads=[bank_b[pb2], rope_b], writes=[qraw_b[qi]])
                            k.op("dve", lambda e: e.tensor_tensor(out=dst[:, c_, t0:t0 + n], in0=t1[ti][:, 0:n], in1=qraw[qi][:, 0:n], op=ALU.add),
                                 reads=[t1_b[ti], qraw_b[qi]], writes=[dst_b[i] for i in tl])
                        jobs.append((st0, st1))
            run_pipe(jobs, 1)
            for i in range(NT):
                pb = 6 + i % 2
                for kc in range(8):
                    mm(bank[pb][:, 0:128], hT[:, kc, i * 128:(i + 1) * 128], wkv[:, kc, 256:384], kc == 0, kc == 7, [wkv_b, hT_b[i], hT_b2[i]], [bank_b[pb]])
                for o_ in (0, 128):
                    k.op("act", lambda e, i=i, pb=pb, o_=o_: e.activation(out=V2[:, i, :, o_:o_ + 64], in_=bank[pb][:, 0:128].rearrange("p (h d) -> p h d", h=2), func=AF.Copy),
                         reads=[bank_b[pb]], writes=[V2_b[i]])
            if l == 0:
                dump("QT", QT, QT_b, BF16)
                dump("KT", KT, KT_b, BF16)
                dump("V2", V2, V2_b, BF16)
            if stop == "B1":
                break

            fence2 = tokens_of([rope_b] + qraw_b + t1_b)
            o2 = o_rope
            NPT, NET = 5, 4
            ptp = [sview(o2 + 512 * i, [128, 2, 512], BF16) for i in range(NPT)]; o2 += 512 * NPT
            etp = [sview(o2 + 512 * i, [128, 2, 512], BF16) for i in range(NET)]; o2 += 512 * NET
            dtot = [sview(o2 + 512 * i, [128, 512], F32) for i in range(2)]; o2 += 1024
            mask2 = sview(o2, [128, 2, 384], BF16); o2 += 384
            assert o2 - O_R3 <= R3W, o2 - O_R3
            ptp_b = [r3buf("ptp%d" % i, fence2) for i in range(NPT)]
            etp_b = [r3buf("etp%d" % i, fence2) for i in range(NET)]
            dtot_b = [r3buf("dtot%d" % i, fence2) for i in range(2)]
            mask2_b = r3buf("mask2", fence2)
            k.dma("pool", lambda e: e.dma_start(out=mask2, in_=dram["k_mask2"]), writes=[mask2_b])
            attnT = brT[0]
            Spair = [PS[i][:, :].rearrange("p (b n) -> p b n", b=2) for i in range(2)]
            sups = [("lat", s_) for s_ in range(4)] + ([("ctx", 0)] if need_ctx else [])
            jobs = []
            cnt_ = {"s": 0, "p": 0, "e": 0, "it": 0, "m": 0}
            for c_ in range(4):
                h = c_ // 2
                for kind, s_ in sups:
                    it = cnt_["it"]; cnt_["it"] += 1
                    Eb = 4 + 2 * (it % 2); Ob = Eb + 1
                    if kind == "lat":
                        q0 = C_CTX + 512 * s_; nq = 512
                    else:
                        q0 = 0; nq = 256
                    qtl = list(range(q0 // 128, (q0 + nq) // 128))
                    kbs = [(kb * 128, kb, 0, nq, None) for kb in range(2)]
                    if kind == "lat":
                        for j in range(4 * s_ - 1, 4 * s_ + 5):
                            if j < 0 or j > 15:
                                continue
                            n_lo = max(j - 1, 4 * s_); n_hi = min(j + 1, 4 * s_ + 3)
                            kbs.append((C_CTX + 128 * j, 2 + j, (n_lo - 4 * s_) * 128, (n_hi - n_lo + 1) * 128, (n_lo - (j - 1)) * 128))
                    for ji, (kcol, vt, qo, qn, mo) in enumerate(kbs):
                        sp = cnt_["s"] % 2; cnt_["s"] += 1
                        pi = cnt_["p"] % NPT; cnt_["p"] += 1
                        first = ji == 0
                        last = ji == len(kbs) - 1

                        def stage0(c_=c_, h=h, kcol=kcol, q0=q0, qo=qo, qn=qn, mo=mo, sp=sp, pi=pi):
                            qb = [QT_b[i] for i in range((q0 + qo) // 128, (q0 + qo + qn) // 128)]
                            for e_ in range(2):
                                ps_ = slice(e_ * 64, (e_ + 1) * 64)
                                mm(Spair[sp][:, e_, 0:qn], KT[ps_, h, kcol:kcol + 128], QT[ps_, c_, q0 + qo:q0 + qo + qn], True, True,
                                   [KT_b[kcol // 128]] + qb, [bank_b[2 * sp + e_]])
                            sb_ = [bank_b[2 * sp], bank_b[2 * sp + 1]]
                            if mo is None:
                                k.op("act", lambda e: e.activation(out=ptp[pi][:, :, 0:qn], in_=Spair[sp][:, :, 0:qn], func=AF.Exp, scale=0.125),
                                     reads=sb_, writes=[ptp_b[pi]])
                            else:
                                ei = cnt_["e"] % NET; cnt_["e"] += 1
                                k.op("act", lambda e: e.activation(out=etp[ei][:, :, 0:qn], in_=Spair[sp][:, :, 0:qn], func=AF.Exp, scale=0.125),
                                     reads=sb_, writes=[etp_b[ei]])
                                meng = "dve" if cnt_["m"] % 4 == 0 else "pool"; cnt_["m"] += 1
                                k.op(meng, lambda e: e.tensor_tensor(out=ptp[pi][:, :, 0:qn], in0=etp[ei][:, :, 0:qn], in1=mask2[:, :, mo:mo + qn], op=ALU.mult),
                                     reads=[etp_b[ei], mask2_b], writes=[ptp_b[pi]])

                        def stage1(c_=c_, h=h, vt=vt, qo=qo, qn=qn, pi=pi, first=first, last=last, Eb=Eb, Ob=Ob, nq=nq, q0=q0, qtl=qtl, it=it):
                            mm(bank[Eb][:, qo:qo + qn], V2[:, vt, h, 0:128], ptp[pi][:, 0, 0:qn], first, last, [V2_b[vt], ptp_b[pi]], [bank_b[Eb]])
                            mm(bank[Ob][:, qo:qo + qn], V2[:, vt, h, 64:192], ptp[pi][:, 1, 0:qn], first, last, [V2_b[vt], ptp_b[pi]], [bank_b[Ob]])
                            if last:
                                di = it % 2
                                lo_, hi_ = slice(0, 64), slice(64, 128)
                                k.op("dve", lambda e: e.tensor_scalar(out=dtot[di][lo_, 0:nq], in0=bank[Eb][hi_, 0:nq], scalar1=esk[hi_, c_:c_ + 1], scalar2=None, op0=ALU.add),
                                     reads=[bank_b[Eb], cb["esk"]], writes=[dtot_b[di]])
                                k.op("dve", lambda e: e.tensor_scalar(out=dtot[di][hi_, 0:nq], in0=bank[Ob][lo_, 0:nq], scalar1=esk[lo_, c_:c_ + 1], scalar2=None, op0=ALU.add),
                                     reads=[bank_b[Ob], cb["esk"]], writes=[dtot_b[di]])
                                k.op("dve", lambda e: e.reciprocal(out=dtot[di][:, 0:nq], in_=dtot[di][:, 0:nq]), reads=[dtot_b[di]], writes=[dtot_b[di]])
                                k.op("dve", lambda e: e.tensor_tensor(out=attnT[lo_, c_, q0:q0 + nq], in0=bank[Eb][lo_, 0:nq], in1=dtot[di][lo_, 0:nq], op=ALU.mult),
                                     reads=[bank_b[Eb], dtot_b[di]], writes=[br_b[0][i] for i in qtl])
                                k.op("dve", lambda e: e.tensor_tensor(out=attnT[hi_, c_, q0:q0 + nq], in0=bank[Ob][hi_, 0:nq], in1=dtot[di][hi_, 0:nq], op=ALU.mult),
                                     reads=[bank_b[Ob], dtot_b[di]], writes=[br_b[0][i] for i in qtl])
                        jobs.append((stage0, stage1))
            run_pipe(jobs, 3)
            if l == 0:
                dump("attnT", attnT, br_b[0], BF16)
            if stop == "B2":
                break

            fence = r3_fence()
            o3 = O_R3
            NP = 2312
            H0 = 1280
            Abuf = sview(o3, [128, NP], F32); o3 += NP
            XCs = [sview(o3 + T * i, [128, T], F32) for i in range(2)]; o3 += 2 * T
            XCB = sview(o3, [128, T], BF16); o3 += T // 2
            Rs = [sview(o3 + H0 * i, [128, H0], F32) for i in range(2)]; o3 += 2 * H0
            Is = [sview(o3 + H0 * i, [128, H0], F32) for i in range(2)]; o3 += 2 * H0
            Ms = [sview(o3 + H0 * i, [128, H0], F32) for i in range(2)]; o3 += 2 * H0
            HS = sview(o3, [128, T], F32); o3 += T
            gy = [sview(o3 + 512 * i, [128, 512], F32) for i in range(2)]; o3 += 1024
            assert o3 - O_R3 <= R3W, o3 - O_R3
            A_b = r3buf("A", fence); XC_bs = [r3buf("XC%d" % i, fence) for i in range(2)]; XCB_b = r3buf("XCB", fence)
            R_bs = [r3buf("R%d" % i, fence) for i in range(2)]; I_bs = [r3buf("I%d" % i, fence) for i in range(2)]; M_bs = [r3buf("M%d" % i, fence) for i in range(2)]
            HS_bs = [r3buf("HS%d" % i, fence) for i in range(NT)]
            gy_b = [r3buf("gy%d" % i, fence) for i in range(2)]
            wrx, wrx_b = wb_next()
            k.dma("pool", lambda e: e.dma_start(out=wrx, in_=w_in[:, RX0:RX0 + 512].rearrange("(k p) n -> p k n", p=128)), writes=[wrx_b])
            wry, wry_b = wb_next()
            k.dma("pool", lambda e: e.dma_start(out=wry, in_=w_in[:, RY0:RY0 + 512].rearrange("(k p) n -> p k n", p=128)), writes=[wry_b])
            k.dma("pool", lambda e: e.dma_start(out=PM, in_=dram["pool_mix"][l].rearrange("g c d -> c g d")), writes=[cb["PM"]])
            k.op("dve", lambda e: e.memset(BD, 0.0), writes=[cb["BD"]])
            for d_ in range(2):
                for gi, gn in enumerate(["lru_w_a", "lru_w_x"]):
                    base = (d_ * 2 + gi) * 4
                    for e_ in range(2):
                        o_ = e_ * 64
                        k.dma("pool", lambda e, d_=d_, gn=gn, base=base, e_=e_, o_=o_: e.dma_start(out=BD[o_:o_ + 64, base:base + 4, o_:o_ + 64],
                                                                                                 in_=dram[gn][l, d_, e_::2].rearrange("h i j -> i h j")), writes=[cb["BD"]])
            rnnT = brT[2]
            poff = lambda t: t + 1 if t < C_CTX else t + 4
            k.op("dve", lambda e: e.memset(Abuf, 0.0), writes=[A_b])
            rrc = {"a": 0, "g": 0, "y": 0}
            hsb = lambda c0, c1: [HS_bs[i] for i in range(c0 // 128, (c1 + 127) // 128)]

            def stPa(cc):
                for (t0, n) in blk_all:
                    pb = rrc["a"] % 2; rrc["a"] += 1
                    tl = list(range(t0 // 128, (t0 + n) // 128))
                    for kc in range(8):
                        mm(bank[pb][:, 0:n], wrx[:, kc, cc * 128:(cc + 1) * 128], hT[:, kc, t0:t0 + n], kc == 0, kc == 7, [wrx_b] + [hT_b[i] for i in tl] + [hT_b2[i] for i in tl], [bank_b[pb]])
                    k.op("act", lambda e: e.activation(out=Abuf[:, poff(t0):poff(t0) + n], in_=bank[pb][:, 0:n], func=AF.Copy), reads=[bank_b[pb]], writes=[A_b])

            def stPb(cc):
                XC = XCs[cc % 2]; XC_b = XC_bs[cc % 2]
                for (o_, ln, oc) in ((1, C_CTX, 0), (260, S_LAT, C_CTX)):
                    k.op("dve", lambda e: e.tensor_scalar(out=XC[:, oc:oc + ln], in0=Abuf[:, o_ - 1:o_ - 1 + ln], scalar1=cwT[:, cc, 0:1], scalar2=cbT[:, cc:cc + 1], op0=ALU.mult, op1=ALU.add),
                         reads=[A_b, cb["cw"]], writes=[XC_b])
                    for tap in range(1, 4):
                        k.op("dve", lambda e, tap=tap: e.scalar_tensor_tensor(out=XC[:, oc:oc + ln], in0=Abuf[:, o_ - 1 + tap:o_ - 1 + tap + ln], scalar=cwT[:, cc, tap:tap + 1], in1=XC[:, oc:oc + ln], op0=ALU.mult, op1=ALU.add),
                             reads=[A_b, XC_b, cb["cw"]], writes=[XC_b])

            def gate_jobs(cc):
                XC = XCs[cc % 2]; XC_b = XC_bs[cc % 2]
                jl = []
                for ji, (d_, c0, c1, sl) in enumerate([(0, 0, H0, 0), (0, H0, T, 1), (1, 0, H0, 0), (1, H0, T, 1)]):
                    R, I_, M = Rs[sl], Is[sl], Ms[sl]
                    R_b, I_b, M_b = R_bs[sl], I_bs[sl], M_bs[sl]
                    w_ = c1 - c0
                    blks = [(t0, n) for (t0, n) in blk_all if c0 <= t0 < c1]

                    def s0(d_=d_, c0=c0, c1=c1, w_=w_, blks=blks, R=R, I_=I_, M=M, R_b=R_b, I_b=I_b, M_b=M_b, ji=ji):
                        if ji == 0:
                            k.op("pool", lambda e: e.tensor_copy(out=XCB, in_=XC), reads=[XC_b], writes=[XCB_b])
                        for gi in range(2):
                            dstg, dstg_b = (R, R_b) if gi == 0 else (I_, I_b)
                            idx = (d_ * 2 + gi) * 4 + cc
                            for (t0, n) in blks:
                                pb = 2 + rrc["g"] % 2; rrc["g"] += 1
                                mm(bank[pb][:, 0:n], BD[:, idx, :], XCB[:, t0:t0 + n], True, True, [cb["BD"], XCB_b], [bank_b[pb]])
                                k.op("act", lambda e, pb=pb, n=n, t0=t0, dstg=dstg, gi=gi: e.activation(out=dstg[:, t0 - c0:t0 - c0 + n], in_=bank[pb][:, 0:n], func=AF.Tanh, scale=0.5, bias=hbT[:, gi, d_, cc:cc + 1]),
                                     reads=[bank_b[pb], cb["c8"]], writes=[dstg_b])
                        k.op("act", lambda e: e.activation(out=M[:, 0:w_], in_=R[:, 0:w_], func=AF.Exp, scale=c8T[:, 1, d_, cc:cc + 1], bias=c8T[:, 1, d_, cc:cc + 1]), reads=[R_b, cb["c8"]], writes=[M_b])
                        k.op("act", lambda e: e.activation(out=R[:, 0:w_], in_=R[:, 0:w_], func=AF.Exp, scale=c8T[:, 0, d_, cc:cc + 1], bias=c8T[:, 0, d_, cc:cc + 1]), reads=[R_b, cb["c8"]], writes=[R_b])
                        k.op("act", lambda e: e.activation(out=M[:, 0:w_], in_=M[:, 0:w_], func=AF.Sqrt, scale=-1.0, bias=1.0), reads=[M_b], writes=[M_b])

                    def s1(d_=d_, c0=c0, c1=c1, w_=w_, R=R, I_=I_, M=M, R_b=R_b, I_b=I_b, M_b=M_b, ji=ji):
                        k.op("dve", lambda e: e.scalar_tensor_tensor(out=I_[:, 0:w_], in0=I_[:, 0:w_], scalar=1.0, in1=XC[:, c0:c1], op0=ALU.add, op1=ALU.mult), reads=[I_b, XC_b], writes=[I_b])
                        k.op("dve", lambda e: e.tensor_tensor(out=I_[:, 0:w_], in0=I_[:, 0:w_], in1=M[:, 0:w_], op=ALU.mult), reads=[I_b, M_b], writes=[I_b])
                        if d_ == 0:
                            init = 0.0 if c0 == 0 else HS[:, c0 - 1:c0]
                            rd = [] if c0 == 0 else [HS_bs[(c0 - 1) // 128]]
                            k.op("dve", lambda e: e.tensor_tensor_scan(out=HS[:, c0:c1], data0=R[:, 0:w_], data1=I_[:, 0:w_], initial=init, op0=ALU.mult, op1=ALU.add),
                                 reads=[R_b, I_b] + rd, writes=hsb(c0, c1))
                        elif c0 == 0:
                            k.op("dve", lambda e: e.tensor_tensor_scan(out=M[:, 0:C_CTX][:, ::-1], data0=R[:, 0:C_CTX][:, ::-1], data1=I_[:, 0:C_CTX][:, ::-1], initial=0.0, op0=ALU.mult, op1=ALU.add),
                                 reads=[R_b, I_b, M_b], writes=[M_b])
                        else:
                            R0, I0, M0 = Rs[0], Is[0], Ms[0]
                            k.op("dve", lambda e: e.tensor_tensor_scan(out=M[:, 0:w_][:, ::-1], data0=R[:, 0:w_][:, ::-1], data1=I_[:, 0:w_][:, ::-1], initial=M0[:, 0:1], op0=ALU.mult, op1=ALU.add),
                                 reads=[R_b, I_b, M_b, M_bs[0]], writes=[M_b])
                            k.op("dve", lambda e: e.tensor_tensor_scan(out=M0[:, C_CTX:H0][:, ::-1], data0=R0[:, C_CTX:H0][:, ::-1], data1=I0[:, C_CTX:H0][:, ::-1], initial=M[:, 0:1], op0=ALU.mult, op1=ALU.add),
                                 reads=[R_bs[0], I_bs[0], M_bs[0], M_b], writes=[M_bs[0]])
                            k.op("dve", lambda e: e.tensor_tensor(out=HS[:, H0:T], in0=HS[:, H0:T], in1=M[:, 0:w_], op=ALU.add), reads=hsb(H0, T) + [M_b], writes=hsb(H0, T))
                            k.op("dve", lambda e: e.tensor_tensor(out=HS[:, 0:H0], in0=HS[:, 0:H0], in1=M0[:, 0:H0], op=ALU.add), reads=hsb(0, H0) + [M_bs[0]], writes=hsb(0, H0))
                    jl.append((s0, s1))
                return jl

            def stY(cc):
                for (t0, n) in blk_out:
                    pb = 4 + rrc["y"] % 2; rrc["y"] += 1
                    gi_ = pb - 4
                    tl = list(range(t0 // 128, (t0 + n) // 128))
                    for kc in range(8):
                        mm(bank[pb][:, 0:n], wry[:, kc, cc * 128:(cc + 1) * 128], hT[:, kc, t0:t0 + n], kc == 0, kc == 7, [wry_b] + [hT_b[i] for i in tl] + [hT_b2[i] for i in tl], [bank_b[pb]])
                    k.op("act", lambda e: e.activation(out=gy[gi_][:, 0:n], in_=bank[pb][:, 0:n], func=AF.Gelu_apprx_tanh), reads=[bank_b[pb]], writes=[gy_b[gi_]])
                    k.op("dve", lambda e: e.scalar_tensor_tensor(out=rnnT[:, cc, t0:t0 + n], in0=HS[:, t0:t0 + n], scalar=0.5, in1=gy[gi_][:, 0:n], op0=ALU.mult, op1=ALU.mult),
                         reads=[HS_bs[i] for i in tl] + [gy_b[gi_]], writes=[br_b[2][i] for i in tl])

            stPa(0)
            stPb(0)
            for cc in range(4):
                jl = gate_jobs(cc)
                nxt = cc + 1 < 4
                jl[0][0]()
                if nxt:
                    stPa(cc + 1)
                jl[1][0]()
                jl[0][1]()
                jl[2][0]()
                jl[1][1]()
                if nxt:
                    stPb(cc + 1)
                jl[3][0]()
                jl[2][1]()
                jl[3][1]()
                stY(cc)
            if l == 0:
                dump("rnnT", rnnT, br_b[2], BF16)
            if stop == "B3":
                break

            fence = r3_fence()
            o3 = O_R3
            NPP = 2336
            PUs = [sview(o3 + NPP * i, [128, NPP], F32) for i in range(2)]; o3 += 2 * NPP
            SAs = [sview(o3 + NPP * i, [128, NPP], F32) for i in range(2)]; o3 += 2 * NPP
            SBs = [sview(o3 + NPP * i, [128, NPP], F32) for i in range(2)]; o3 += 2 * NPP
            Dbs = [sview(o3 + (NPP // 2) * i, [128, NPP], BF16) for i in range(2)]; o3 += NPP
            etmps = [sview(o3 + 32 * i, [128, 32], F32) for i in range(2)]; o3 += 64
            assert o3 - O_R3 <= R3W, o3 - O_R3
            PU_bs = [r3buf("PU%d" % i, fence) for i in range(2)]; SA_bs = [r3buf("SA%d" % i, fence) for i in range(2)]
            SB_bs = [r3buf("SB%d" % i, fence) for i in range(2)]
            D_bs = [[r3buf("D%d_%d" % (i, j), fence) for j in range(5)] for i in range(2)]
            et2_bs = [[r3buf("etmp%d_%d" % (i, j), fence) for j in range(4)] for i in range(2)]
            wpu, wpu_b = wb_next()
            k.dma("pool", lambda e: e.dma_start(out=wpu, in_=w_in[:, PU0:PU0 + 512].rearrange("(k p) n -> p k n", p=128)), writes=[wpu_b])
            poolT = brT[1]
            qoff = lambda t: t + 8 if t < C_CTX else t + 24
            for i_ in range(2):
                k.op("dve", lambda e, i_=i_: e.memset(PUs[i_], 0.0), writes=[PU_bs[i_]])
            rrp = {"a": 0, "m": 0}
            jobs = []
            for g in range(4):
                w = 2 << g
                pa_ = g % 2
                PU, PU_b, SA, SA_b, SBf, SB_b, Dbf, D_b, etmp, et2_b = PUs[pa_], PU_bs[pa_], SAs[pa_], SA_bs[pa_], SBs[pa_], SB_bs[pa_], Dbs[pa_], D_bs[pa_], etmps[pa_], et2_bs[pa_]

                def stP(g=g, PU=PU, PU_b=PU_b):
                    for (t0, n) in blk_out:
                        pb = rrp["a"] % 2; rrp["a"] += 1
                        tl = list(range(t0 // 128, (t0 + n) // 128))
                        for kc in range(8):
                            mm(bank[pb][:, 0:n], wpu[:, kc, g * 128:(g + 1) * 128], hT[:, kc, t0:t0 + n], kc == 0, kc == 7, [wpu_b] + [hT_b[i] for i in tl] + [hT_b2[i] for i in tl], [bank_b[pb]])
                        k.op("act", lambda e: e.activation(out=PU[:, qoff(t0):qoff(t0) + n], in_=bank[pb][:, 0:n], func=AF.Copy), reads=[bank_b[pb]], writes=[PU_b])

                def stD(g=g, w=w, PU=PU, PU_b=PU_b, SA=SA, SA_b=SA_b, SBf=SBf, SB_b=SB_b, Dbf=Dbf, D_b=D_b, etmp=etmp, et2_b=et2_b):
                    eng = "pool" if g % 2 == 0 else "dve"
                    k.op(eng, lambda e: e.tensor_tensor(out=SA[:, 0:NPP - 1], in0=PU[:, 0:NPP - 1], in1=PU[:, 1:NPP], op=ALU.add), reads=[PU_b], writes=[SA_b])
                    cur, cur_b, oth, oth_b = SA, SA_b, SBf, SB_b
                    lo, hi = 0, NPP - 1
                    sh = 1
                    for step in range(g):
                        nlo, nhi = lo + sh, hi - sh
                        k.op(eng, lambda e, cur=cur, oth=oth, nlo=nlo, nhi=nhi, sh=sh: e.tensor_tensor(out=oth[:, nlo:nhi], in0=cur[:, nlo - sh:nhi - sh], in1=cur[:, nlo + sh:nhi + sh], op=ALU.add),
                             reads=[cur_b], writes=[oth_b])
                        cur, cur_b, oth, oth_b = oth, oth_b, cur, cur_b
                        lo, hi = nlo, nhi
                        sh *= 2
                    k.op("dve", lambda e: e.scalar_tensor_tensor(out=Dbf[:, lo:hi], in0=cur[:, lo:hi], scalar=1.0 / w, in1=PU[:, lo:hi], op0=ALU.mult, op1=ALU.subtract),
                         reads=[cur_b, PU_b], writes=D_b)
                    segs = ((8, C_CTX), (280, S_LAT)) if need_ctx else ((280, S_LAT),)
                    edges = [(o_ if side == 0 else o_ + ln - 8, side) for (o_, ln) in segs for side in range(2)]
                    for ei_, (c0, side) in enumerate(edges):
                        k.op(eng, lambda e, c0=c0, side=side, ei_=ei_: e.tensor_tensor(out=etmp[:, ei_ * 8:(ei_ + 1) * 8], in0=cur[:, c0:c0 + 8], in1=ecT[:, (g * 2 + side) * 8:(g * 2 + side) * 8 + 8], op=ALU.mult),
                             reads=[cur_b, cb["ec"]], writes=[et2_b[ei_]])
                    for ei_, (c0, side) in enumerate(edges):
                        k.op(eng, lambda e, c0=c0, ei_=ei_: e.tensor_tensor(out=Dbf[:, c0:c0 + 8], in0=etmp[:, ei_ * 8:(ei_ + 1) * 8], in1=PU[:, c0:c0 + 8], op=ALU.subtract),
                             reads=[et2_b[ei_], PU_b], writes=[D_b[1 + ei_]])

                def stM(g=g, Dbf=Dbf, D_b=D_b):
                    for (t0, n) in blk_out:
                        pb = 2 + rrp["m"] % 2; rrp["m"] += 1
                        tl = list(range(t0 // 128, (t0 + n) // 128))
                        mm(bank[pb][:, 0:n], PM[:, g, :], Dbf[:, qoff(t0):qoff(t0) + n], True, True, [cb["PM"]] + D_b, [bank_b[pb]])
                        k.op("act", lambda e: e.activation(out=poolT[:, g, t0:t0 + n], in_=bank[pb][:, 0:n], func=AF.Copy, scale=psT[:, g:g + 1]),
                             reads=[bank_b[pb], cb["ps"]], writes=[br_b[1][i] for i in tl])
                jobs.append((stP, stD, stM))
            run_pipe(jobs, 1)
            if l == 0:
                dump("poolT", poolT, br_b[1], BF16)
            if stop == "B4":
                break

            fence = r3_fence()
            o3 = O_R3
            MG = sview(o3, [128, 8, T], BF16); o3 += 9216
            WG = [sview(o3 + 2304 * i, [128, 8, 3, 128], BF16) for i in range(2)]
            WJ = [sview(o3 + 2304 * i + 1536, [128, 3, 4, 128], BF16) for i in range(2)]; o3 += 4608
            gs = [sview(o3 + 512 * i, [128, 512], F32) for i in range(2)]; o3 += 1024
            acc = [sview(o3 + 512 * i, [128, 512], F32) for i in range(2)]; o3 += 1024
            tt_ = [sview(o3 + 512 * i, [128, 512], F32) for i in range(2)]; o3 += 1024
            assert o3 - O_R3 <= R3W, o3 - O_R3
            MG_b = [r3buf("MG%d" % i, fence) for i in range(NT)]
            WGJ_b = [r3buf("WGJ%d" % i, fence) for i in range(2)]
            gs_b = [r3buf("gs%d" % i, fence) for i in range(2)]
            acc_b = [r3buf("acc%d" % i, fence) for i in range(2)]
            tt_b = [r3buf("tt%d" % i, fence) for i in range(2)]
            wjo = [dram["w_attn_o"][l], dram["w_pool_o"][l], dram["w_rnn_o"][l]]

            def load_m(m):
                s_ = m % 2
                for j in range(3):
                    k.dma("pool", lambda e, j=j, m=m, s_=s_: e.dma_start(out=WG[s_][:, :, j, :], in_=w_in[:, GL0 + j * 1024 + m * 128:GL0 + j * 1024 + (m + 1) * 128].rearrange("(k p) n -> p k n", p=128)), writes=[WGJ_b[s_]])
                    k.dma("pool", lambda e, j=j, m=m, s_=s_: e.dma_start(out=WJ[s_][:, j, :, :], in_=wjo[j][:, m * 128:(m + 1) * 128].rearrange("(k p) n -> p k n", p=128)), writes=[WGJ_b[s_]])
            load_m(0)
            gi_ = 0
            ai = 0
            for m in range(8):
                if m + 1 < 8:
                    load_m(m + 1)
                s_ = m % 2
                for (t0, n) in blk_out:
                    tl = list(range(t0 // 128, (t0 + n) // 128))
                    a_ = ai % 2; ai += 1
                    for j in range(3):
                        pg = (rr % 2) * 2; py = pg + 1; rr += 1
                        for kc in range(8):
                            mm(bank[pg][:, 0:n], WG[s_][:, kc, j, :], hT[:, kc, t0:t0 + n], kc == 0, kc == 7, [WGJ_b[s_]] + [hT_b[i] for i in tl] + [hT_b2[i] for i in tl], [bank_b[pg]])
                        for kc in range(4):
                            mm(bank[py][:, 0:n], WJ[s_][:, j, kc, :], brT[j][:, kc, t0:t0 + n], kc == 0, kc == 3, [WGJ_b[s_]] + [br_b[j][i] for i in tl], [bank_b[py]])
                        g_ = gi_ % 2; gi_ += 1
                        k.op("act", lambda e, pg=pg, n=n, g_=g_: e.activation(out=gs[g_][:, 0:n], in_=bank[pg][:, 0:n], func=AF.Sigmoid), reads=[bank_b[pg]], writes=[gs_b[g_]])
                        if j == 0:
                            k.op("dve", lambda e, py=py, n=n, g_=g_, a_=a_: e.tensor_tensor(out=acc[a_][:, 0:n], in0=bank[py][:, 0:n], in1=gs[g_][:, 0:n], op=ALU.mult),
                                 reads=[bank_b[py], gs_b[g_]], writes=[acc_b[a_]])
                        else:
                            k.op("dve", lambda e, py=py, n=n, g_=g_: e.tensor_tensor(out=tt_[g_][:, 0:n], in0=bank[py][:, 0:n], in1=gs[g_][:, 0:n], op=ALU.mult),
                                 reads=[bank_b[py], gs_b[g_]], writes=[tt_b[g_]])
                            if j == 1:
                                k.op("dve", lambda e, n=n, g_=g_, a_=a_: e.tensor_tensor(out=acc[a_][:, 0:n], in0=acc[a_][:, 0:n], in1=tt_[g_][:, 0:n], op=ALU.add),
                                     reads=[acc_b[a_], tt_b[g_]], writes=[acc_b[a_]])
                            else:
                                k.op("dve", lambda e, n=n, g_=g_, a_=a_, m=m, t0=t0: e.tensor_tensor(out=MG[:, m, t0:t0 + n], in0=acc[a_][:, 0:n], in1=tt_[g_][:, 0:n], op=ALU.add),
                                     reads=[acc_b[a_], tt_b[g_]], writes=[MG_b[i] for i in tl])
            if l == 0:
                dump("MG", MG, MG_b, BF16)
            if stop == "B5":
                break

            o3 = O_R3 + 9216
            NX6, NS6 = 3, 3
            GB = [sview(o3 + 1024 * i, [128, 1024], F32) for i in range(2)]; o3 += 2048
            xt6 = [sview(o3 + 1024 * i, [128, 1024], F32) for i in range(NX6)]; o3 += 1024 * NX6
            xs6 = [sview(o3 + 1024 * i, [128, 1024], F32) for i in range(NS6)]; o3 += 1024 * NS6
            dg = sview(o3, [128, 128], F32); o3 += 128
            onesf = sview(o3, [128, 128], F32); o3 += 128
            assert o3 - O_R3 <= R3W, o3 - O_R3
            fence6 = tokens_of(WGJ_b + gs_b + acc_b + tt_b)
            GB_b = [Buf("GB%d" % i, fence6) for i in range(2)]
            xt6_b = [Buf("xt6_%d" % i, fence6) for i in range(NX6)]
            xs6_b = [Buf("xs6_%d" % i, fence6) for i in range(NS6)]
            dg_b = Buf("dg", fence6)
            onesf_b = Buf("onesf", fence6)
            r3_prev.extend(GB_b + xt6_b + xs6_b + [dg_b, onesf_b])

            def build_GB(Gc):
                for r in range(2 if need_ctx else 1):
                    for kc in range(8):
                        k.op("dve", lambda e, kc=kc, r=r: e.tensor_scalar(out=dg, in0=ident, scalar1=Gc[:, kc, r:r + 1], scalar2=None, op0=ALU.mult),
                             reads=[cb["ident"], AG_b[l]], writes=[dg_b])
                        pb = 4 + kc // 4
                        k.op("pe", lambda e, kc=kc, pb=pb: e.matmul(bank[pb][:, (kc % 4) * 128:(kc % 4) * 128 + 128], lhsT=onesf, rhs=dg, start=True, stop=True),
                             reads=[dg_b, onesf_b], writes=[bank_b[pb]])
                    for half in range(2):
                        k.op("act", lambda e, half=half, r=r: e.activation(out=GB[r][:, half * 512:(half + 1) * 512], in_=bank[4 + half], func=AF.Copy),
                             reads=[bank_b[4 + half]], writes=[GB_b[r]])
            k.op("dve", lambda e: e.memset(onesf, 1.0), writes=[onesf_b])
            build_GB(G2)
            wo = [None, None]; wo_b = [None, None]
            for half in range(2):
                wo[half], wo_b[half] = wb_next()
                k.dma("pool", lambda e, half=half: e.dma_start(out=wo[half], in_=dram["w_out"][l][:, half * 512:(half + 1) * 512].rearrange("(k p) n -> p k n", p=128)), writes=[wo_b[half]])

            def post_update(ps_pair, ps_bufs, xt_ap, xt_buf, tmp_ap, tmp_buf, r):
                rs, rs_b = rstd_of(ps_pair, ps_bufs)
                k.op("dve", lambda e: e.scalar_tensor_tensor(out=tmp_ap, in0=ps_pair, scalar=rs, in1=GB[r], op0=ALU.mult, op1=ALU.mult),
                     reads=ps_bufs + [rs_b, GB_b[r]], writes=[tmp_buf])
                k.op("pool", lambda e: e.tensor_tensor(out=xt_ap, in0=xt_ap, in1=tmp_ap, op=ALU.add), reads=[tmp_buf, xt_buf], writes=[xt_buf])

            jobs = []
            for ii, i in enumerate(tiles):
                def st0(ii=ii, i=i):
                    s_ = ii % NX6
                    pp = ii % 3
                    k.dma("sp", lambda e: e.dma_start(out=xt6[s_], in_=src_ap(i)), reads=[src_b[i]], writes=[xt6_b[s_]])
                    for half in range(2):
                        for kc in range(8):
                            mm(pair[pp][:, half * 512:(half + 1) * 512], MG[:, kc, i * 128:(i + 1) * 128], wo[half][:, kc, :], kc == 0, kc == 7, [MG_b[i], wo_b[half]], [bank_b[2 * pp + half]])

                def st1(ii=ii, i=i):
                    s_ = ii % NX6
                    pp = ii % 3
                    x2 = ii % NS6
                    r = 1 if i < 2 else 0
                    post_update(pair[pp], [bank_b[2 * pp], bank_b[2 * pp + 1]], xt6[s_], xt6_b[s_], xs6[x2], xs6_b[x2], r)
                    k.dma("sp", lambda e: e.dma_start(out=xa[i * 128:(i + 1) * 128, :], in_=xt6[s_]), reads=[xt6_b[s_]], writes=[xa_b[i]])

                def st2(ii=ii, i=i):
                    s_ = ii % NX6
                    x2 = ii % NS6
                    norm_stats(xt6[s_], xt6_b[s_], xs6[x2], xs6_b[x2])

                def st3(ii=ii, i=i):
                    x2 = ii % NS6
                    r = 1 if i < 2 else 0
                    norm_transpose(xs6[x2], xs6_b[x2], A1, B1, coef_b, r, i * 128, (6, 7))
                jobs.append((st0, st1, st2, st3))
            run_pipe(jobs, 1)
            if l == 0:
                dump("h2T", hT, hT_b + hT_b2, BF16)
            if stop == "B6":
                break

            fence = r3_fence()
            fenceR2 = tokens_of([b_ for row in br_b for b_ in row])
            o3 = O_R3
            WD = sview(o3, [128, NM, 1024], BF16); o3 += 11264
            GBf = [sview(o3 + 1024 * i, [128, 1024], F32) for i in range(2)]; o3 += 2048
            xt7 = [sview(o3 + 1024 * i, [128, 1024], F32) for i in range(2)]; o3 += 2048
            sg = [sview(o3 + 512 * i, [128, 512], F32) for i in range(2)]; o3 += 1024
            dg = sview(o3, [128, 128], F32); o3 += 128
            onesf = sview(o3, [128, 128], F32); o3 += 128
            assert o3 - O_R3 <= R3W, o3 - O_R3
            WD_b = r3buf("WD", fence)
            GB = GBf
            GB_b = [r3buf("GBf%d" % i, fence) for i in range(2)]
            xt7_b = [r3buf("xt7_%d" % i, fence) for i in range(2)]
            sg_b = [r3buf("sg%d" % i, fence) for i in range(2)]
            dg_b = r3buf("dgf", fence)
            onesf_b = r3buf("onesff", fence)
            k.op("dve", lambda e: e.memset(onesf, 1.0), writes=[onesf_b])
            build_GB(G5)
            lo_t = 0 if need_ctx else C_CTX
            half_t = (T - lo_t) // 2
            groups = [(lo_t, lo_t + half_t), (lo_t + half_t, T)]
            GW = half_t
            actT = sview(O_R2, [128, NM, GW], BF16)
            act_b = [Buf("act%d" % i, fenceR2) for i in range(GW // 128)]
            for b_ in [b2 for row in br_b for b2 in row]:
                b_.w = None; b_.r = {}
            w_gu = dram["w_gu"][l]
            dst_ap = (lambda i: xb[i * 128:(i + 1) * 128, :]) if need_ctx else (lambda i: y[(i - 2) * 128:(i - 1) * 128, :])
            dst_b = xb_b if need_ctx else y_b
            for q4 in range(4):
                m0 = q4 * 6
                m1 = min(NM, m0 + 6)
                k.dma("pool", lambda e, m0=m0, m1=m1: e.dma_start(out=WD[:, m0:m1, :], in_=dram["w_down"][l][m0 * 128:m1 * 128, :].rearrange("(m p) n -> p m n", p=128)), writes=[WD_b])
            for (g0, g1) in groups:
                gblks = blocks_of(g0, g1)
                for mq in range(6):
                    m0 = mq * 4
                    nm_ = min(4, NM - m0)
                    wg_, wg_b = wb_next()
                    k.dma("pool", lambda e, m0=m0, nm_=nm_, wg_=wg_: e.dma_start(out=wg_[:, :, 0:nm_ * 128], in_=w_gu[:, m0 * 128:(m0 + nm_) * 128].rearrange("(k p) n -> p k n", p=128)), writes=[wg_b])
                    wu_, wu_b = wb_next()
                    k.dma("pool", lambda e, m0=m0, nm_=nm_, wu_=wu_: e.dma_start(out=wu_[:, :, 0:nm_ * 128], in_=w_gu[:, D_FF + m0 * 128:D_FF + (m0 + nm_) * 128].rearrange("(k p) n -> p k n", p=128)), writes=[wu_b])
                    for mi in range(nm_):
                        m = m0 + mi
                        for (t0, n) in gblks:
                            tl = list(range(t0 // 128, (t0 + n) // 128))
                            pg = (rr % 2) * 2; pu_ = pg + 1; rr += 1
                            for kc in range(8):
                                mm(bank[pg][:, 0:n], wg_[:, kc, mi * 128:(mi + 1) * 128], hT[:, kc, t0:t0 + n], kc == 0, kc == 7, [wg_b] + [hT_b[i] for i in tl] + [hT_b2[i] for i in tl], [bank_b[pg]])
                            for kc in range(8):
                                mm(bank[pu_][:, 0:n], wu_[:, kc, mi * 128:(mi + 1) * 128], hT[:, kc, t0:t0 + n], kc == 0, kc == 7, [wu_b] + [hT_b[i] for i in tl] + [hT_b2[i] for i in tl], [bank_b[pu_]])
                            g_ = gi_ % 2; gi_ += 1
                            k.op("act", lambda e, pg=pg, n=n, g_=g_: e.activation(out=sg[g_][:, 0:n], in_=bank[pg][:, 0:n], func=AF.Silu), reads=[bank_b[pg]], writes=[sg_b[g_]])
                            k.op("dve", lambda e, pu_=pu_, n=n, g_=g_, m=m, t0=t0, g0=g0: e.tensor_tensor(out=actT[:, m, t0 - g0:t0 - g0 + n], in0=bank[pu_][:, 0:n], in1=sg[g_][:, 0:n], op=ALU.mult),
                                 reads=[bank_b[pu_], sg_b[g_]], writes=[act_b[(i * 128 - g0) // 128] for i in tl])
                for ii, i in enumerate(range(g0 // 128, g1 // 128)):
                    s_ = ii % 2
                    r = 1 if i < 2 else 0
                    k.dma("sp", lambda e, i=i, s_=s_: e.dma_start(out=xt7[s_], in_=xa[i * 128:(i + 1) * 128, :]), reads=[xa_b[i]], writes=[xt7_b[s_]])
                    pp = 2 + s_
                    ia = i - g0 // 128
                    for half in range(2):
                        for m in range(NM):
                            mm(pair[pp][:, half * 512:(half + 1) * 512], actT[:, m, ia * 128:(ia + 1) * 128], WD[:, m, half * 512:(half + 1) * 512], m == 0, m == NM - 1,
                               [act_b[ia], WD_b], [bank_b[2 * pp + half]])
                    tmpf = sview(O_R3 + 11264 + 2048 + 2048, [128, 1024], F32)
                    rs, rs_b = rstd_of(pair[pp], [bank_b[2 * pp], bank_b[2 * pp + 1]])
                    k.op("dve", lambda e, pp=pp, r=r, tmpf=tmpf, rs=rs: e.scalar_tensor_tensor(out=tmpf, in0=pair[pp], scalar=rs, in1=GB[r], op0=ALU.mult, op1=ALU.mult),
                         reads=[bank_b[2 * pp], bank_b[2 * pp + 1], rs_b, GB_b[r]], writes=[sg_b[0], sg_b[1]])
                    k.op("pool", lambda e, s_=s_, tmpf=tmpf: e.tensor_tensor(out=xt7[s_], in0=xt7[s_], in1=tmpf, op=ALU.add), reads=[xt7_b[s_], sg_b[0], sg_b[1]], writes=[xt7_b[s_]])
                    k.dma("sp", lambda e, i=i, s_=s_: e.dma_start(out=dst_ap(i), in_=xt7[s_]), reads=[xt7_b[s_]], writes=[dst_b[i]])
            r3_prev.extend(act_b)
            if stop == "L%d" % l:
                break

        outs = list(dump_out.values())
        if stop is None:
            outs += y_b[2:]
        k.wait_all("sp", outs)
        k.emit(st)
    return nc


def kernel(**inputs):
    consts = make_consts()
    vecs = make_vecs(inputs)
    nc = build()
    B = inputs["x"].shape[0]
    in_maps = []
    for b in range(B):
        m = {"x": np.ascontiguousarray(inputs["x"][b], dtype=np.float32),
             "ctx": np.ascontiguousarray(inputs["ctx"][b], dtype=np.float32),
             "cvecT": np.ascontiguousarray(np.stack([inputs["c"][b], inputs["c_ctx"]], 0).astype(np.float32).reshape(2, 8, 128).transpose(2, 1, 0)),
             "vecs": vecs}
        for n in WEIGHT_NAMES:
            m[n] = np.ascontiguousarray(inputs[n], dtype=np.float32)
        for n in CONST_SHAPES:
            m["k_" + n] = consts[n]
        in_maps.append(m)
    res = run_bass_kernel_spmd(nc, in_maps, core_ids=list(range(B)))
    return np.stack([np.asarray(r["y"]) for r in res.results], 0).astype(np.float32)
```

```python
import numpy as np
from contextlib import ExitStack
import concourse.bass as bass
import concourse.mybir as mybir
from concourse.bass_utils import run_bass_kernel_spmd

F32 = mybir.dt.float32
BF16 = mybir.dt.bfloat16
AF = mybir.ActivationFunctionType
ALU = mybir.AluOpType

D = 1024
S_LAT = 2048
C_CTX = 256
T = S_LAT + C_CTX
NT = T // 128
DEPTH = 2
IN_W = 5376
D_FF = 2816
NM = D_FF // 128
EPS = 1e-6
Q0, K0, V0, RX0, RY0, PU0, GL0 = 0, 512, 640, 768, 1280, 1792, 2304


class Buf:
    __slots__ = ("name", "w", "r", "excl")

    def __init__(self, name="", fence=None, excl=False):
        self.name = name
        self.excl = excl
        self.w = None
        self.r = dict(fence) if fence else {}


def tokens_of(bufs):
    d = {}
    for b in bufs:
        if b.w is not None and d.get(b.w[0], 0) < b.w[1]:
            d[b.w[0]] = b.w[1]
        for k, v in b.r.items():
            if d.get(k, 0) < v:
                d[k] = v
    return d


class _Rec:
    def __init__(self):
        self.call = None

    def __getattr__(self, name):
        def f(*a, **kw):
            self.call = (name, a, kw)
            return self
        return f


def _record(fn):
    r = _Rec()
    fn(r)
    assert r.call is not None
    return r.call


class K:
    ENGS = ("pe", "act", "dve", "pool", "sp")
    NO_SELF_SYNC = ("pe", "sp")

    def __init__(self, nc, n_dma_sems=32):
        self.nc = nc
        self.prog = {e: [] for e in self.ENGS}
        self.cnt = {e: 0 for e in self.ENGS}
        self.seen = {e: {} for e in self.ENGS}
        self.n_dma_sems = n_dma_sems
        self.dma_cum = [0] * n_dma_sems
        self.dma_next = [0, 0]
        self.sems = {}

    def _deps(self, eng, reads, writes, relax_self=False):
        deps = {}
        raw_self = 0

        def add(k, v):
            if deps.get(k, 0) < v:
                deps[k] = v
        for b in reads:
            if b.w is not None:
                if b.w[0] == eng:
                    raw_self = max(raw_self, b.w[1])
                else:
                    add(*b.w)
            if b.excl:
                for k, v in b.r.items():
                    if k != eng:
                        add(k, v)
        for b in writes:
            if b.w is not None:
                if b.w[0] != eng or not relax_self:
                    add(*b.w)
            for k, v in b.r.items():
                if k != eng or not relax_self:
                    add(k, v)
        if raw_self:
            add(eng, raw_self)
        out = []
        seen = self.seen[eng]
        for k, v in deps.items():
            if k == eng and eng in self.NO_SELF_SYNC:
                continue
            if seen.get(k, 0) >= v:
                continue
            seen[k] = v
            out.append((k, v))
        return out

    @staticmethod
    def _mark(tok, reads, writes):
        k, v = tok
        for b in reads:
            if b.r.get(k, 0) < v:
                b.r[k] = v
        for b in writes:
            b.w = tok
            b.r = {}

    def op(self, eng, fn, reads=(), writes=()):
        waits = self._deps(eng, reads, writes, relax_self=False)
        self.cnt[eng] += 1
        tok = (eng, self.cnt[eng])
        self._mark(tok, reads, writes)
        self.prog[eng].append((waits, _record(fn), (eng, 1)))

    def dma(self, eng, fn, reads=(), writes=()):
        half = self.n_dma_sems // 2
        qi = 0 if eng == "sp" else 1
        i = qi * half + self.dma_next[qi]
        self.dma_next[qi] = (self.dma_next[qi] + 1) % half
        key = ("dma", i)
        waits = self._deps(eng, reads, writes)
        if self.dma_cum[i] > 0 and self.seen[eng].get(key, 0) < self.dma_cum[i]:
            self.seen[eng][key] = self.dma_cum[i]
            waits.append((key, self.dma_cum[i]))
        self.dma_cum[i] += 16
        tok = (key, self.dma_cum[i])
        self._mark(tok, reads, writes)
        self.prog[eng].append((waits, _record(fn), (key, 16)))

    def wait_all(self, eng, bufs):
        waits = self._deps(eng, bufs, ())
        self.prog[eng].append((waits, None, None))

    def emit(self, stack):
        nc = self.nc
        for e in self.ENGS:
            self.sems[e] = stack.enter_context(nc.semaphore("s_" + e))
        for i in range(self.n_dma_sems):
            self.sems[("dma", i)] = stack.enter_context(nc.semaphore("s_dma%d" % i))
        block = stack.enter_context(nc.Block())
        sems = self.sems

        def mk(e):
            def body(engine):
                for waits, fn, inc in self.prog[e]:
                    for k, v in waits:
                        engine.wait_ge(sems[k], v)
                    if fn is not None:
                        name, a, kw = fn
                        getattr(engine, name)(*a, **kw).then_inc(sems[inc[0]], inc[1])
            return body
        block.tensor(mk("pe"))
        block.scalar(mk("act"))
        block.vector(mk("dve"))
        block.gpsimd(mk("pool"))
        block.sync(mk("sp"))


def make_consts():
    c = {}
    c["ident"] = np.eye(128, dtype=np.float32)
    R = np.zeros((128, 128), np.float32)
    for m in range(128):
        partner = m + 16 if (m % 32) < 16 else m - 16
        R[partner, m] = 1.0
    c["rmat"] = R
    t = np.arange(S_LAT)
    row = (t // 64).astype(np.float32)
    col = (t % 64).astype(np.float32)
    inv = (np.float32(10000.0) ** (-(np.arange(16, dtype=np.float32) * np.float32(2.0) / np.float32(32)))).astype(np.float32)
    cosT = np.zeros((128, S_LAT), np.float32)
    sinT = np.zeros((128, S_LAT), np.float32)
    for p in range(128):
        j = p % 64
        f = j % 16
        ang = ((row if j < 32 else col) * inv[f]).astype(np.float32)
        sign = -1.0 if (j % 32) < 16 else 1.0
        cosT[p] = np.cos(ang)
        sinT[p] = sign * np.sin(ang)
    c["rope"] = np.stack([cosT, sinT], 1).copy()
    b = np.arange(128)[:, None]
    a = np.arange(128)[None, :]
    mask = np.ones((128, 384), np.float32)
    mask[:, 0:128] = (b <= a)
    mask[:, 256:384] = (a <= b)
    c["mask"] = mask
    c["mask2"] = np.stack([mask, mask], 1).copy()
    ones2 = np.zeros((128, 192), np.float32)
    ones2[:, 0:64] = 1.0
    ones2[:, 128:192] = 1.0
    c["ones2"] = ones2
    ec = np.zeros((128, 4, 2, 8), np.float32)
    L = 256
    for g, w in enumerate((2, 4, 8, 16)):
        for i in range(8):
            tt = i
            lo = max(tt - (w - 1) // 2, 0); hi = min(tt + w // 2 + 1, L)
            ec[:, g, 0, i] = 1.0 / (hi - lo)
            tt = L - 8 + i
            lo = max(tt - (w - 1) // 2, 0); hi = min(tt + w // 2 + 1, L)
            ec[:, g, 1, i] = 1.0 / (hi - lo)
    c["ec"] = ec.reshape(128, 64)
    return c


WEIGHT_NAMES = ["w_ada", "b_ada", "w_in", "w_attn_o", "pool_mix", "w_pool_o", "lru_w_a", "lru_w_x", "w_rnn_o", "w_out", "w_gu", "w_down"]
NV = 84


def make_vecs(inputs):
    out = np.zeros((DEPTH, 128, NV), np.float32)
    pc = lambda v: np.asarray(v, np.float32).reshape(-1, 128).T
    for l in range(DEPTH):
        t = out[l]
        for gi, gn in enumerate(["g_pre_mix", "g_post_mix", "g_pre_ffn", "g_post_ffn"]):
            t[:, gi * 8:(gi + 1) * 8] = pc(inputs[gn][l])
        for tap in range(4):
            t[:, 32 + tap:48:4] = pc(inputs["conv_w"][l, tap])
        t[:, 48:52] = pc(inputs["conv_b"][l])
        for vi, vn in enumerate(["lru_b_a", "lru_b_x", "lru_lambda"]):
            for d_ in range(2):
                t[:, 52 + vi * 8 + d_ * 4:52 + vi * 8 + d_ * 4 + 4] = pc(inputs[vn][l, d_])
        t[:, 76:80] = pc(inputs["pool_scale"][l])
        sk = np.asarray(inputs["attn_sink"][l], np.float32).reshape(4, 2)
        t[0:64, 80:84] = sk[:, 1][None, :]
        t[64:128, 80:84] = sk[:, 0][None, :]
    return out


WEIGHT_SHAPES = {
    "w_ada": [2, 1024, 6144], "b_ada": [2, 6144], "w_in": [2, 1024, 5376],
    "w_attn_o": [2, 512, 1024], "pool_mix": [2, 4, 128, 128], "w_pool_o": [2, 512, 1024],
    "lru_w_a": [2, 2, 8, 64, 64], "lru_w_x": [2, 2, 8, 64, 64], "w_rnn_o": [2, 512, 1024],
    "w_out": [2, 1024, 1024], "w_gu": [2, 1024, 5632], "w_down": [2, 2816, 1024],
}
CONST_SHAPES = {"ident": [128, 128], "rmat": [128, 128], "rope": [128, 2, 2048], "mask": [128, 384], "mask2": [128, 2, 384],
                "ones2": [128, 192], "ec": [128, 64]}


def blocks_of(lo, hi):
    out = []
    t = lo
    while t < hi:
        lim = C_CTX if t < C_CTX else hi
        n = min(512, lim - t, hi - t)
        out.append((t, n))
        t += n
    return out


def run_pipe(jobs, lag):
    n = len(jobs)
    nst = max(len(j) for j in jobs) if jobs else 0
    for step in range(n + (nst - 1) * lag):
        for st_ in range(nst):
            i = step - st_ * lag
            if 0 <= i < n and st_ < len(jobs[i]):
                jobs[i][st_]()


def build(stop=None, dumps=()):
    nc = bass.Bass("TRN2", target_bir_lowering=False)
    dram = {}
    dram["x"] = nc.dram_tensor("x", [S_LAT, D], F32, kind="ExternalInput").ap()
    dram["ctx"] = nc.dram_tensor("ctx", [C_CTX, D], F32, kind="ExternalInput").ap()
    dram["cvecT"] = nc.dram_tensor("cvecT", [128, 8, 2], F32, kind="ExternalInput").ap()
    dram["vecs"] = nc.dram_tensor("vecs", [DEPTH, 128, NV], F32, kind="ExternalInput").ap()
    for n in WEIGHT_NAMES:
        dram[n] = nc.dram_tensor(n, WEIGHT_SHAPES[n], F32, kind="ExternalInput").ap()
    for n, shp in CONST_SHAPES.items():
        dram["k_" + n] = nc.dram_tensor("k_" + n, shp, F32, kind="ExternalInput").ap()
    y = nc.dram_tensor("y", [S_LAT, D], F32, kind="ExternalOutput").ap()
    xa = nc.dram_tensor("xa", [T, D], F32, kind="Internal").ap()
    xb = nc.dram_tensor("xb", [T, D], F32, kind="Internal").ap()
    dump_out = {}

    with ExitStack() as st:
        NWORDS = 53200
        R3W = 19184
        S = st.enter_context(nc.sbuf_tensor("S", [128, NWORDS], F32))
        PS = [st.enter_context(nc.psum_tensor("ps%d" % i, [128, 1024], F32)) for i in range(4)]
        k = K(nc)

        def sview(off, shape, dt):
            n = int(np.prod(shape[1:]))
            nw = n if dt == F32 else (n + 1) // 2
            ap = S[:, off:off + nw]
            if dt == BF16:
                ap = ap.bitcast(BF16)
            if len(shape) == 3:
                ap = ap.rearrange("p (a b) -> p a b", a=shape[1])
            elif len(shape) == 4:
                ap = ap.rearrange("p (a b c) -> p a b c", a=shape[1], b=shape[2])
            return ap

        off = 0

        def take(nw):
            nonlocal off
            o = off
            off += nw
            return o
        O_R1 = take(9216)
        O_R2 = take(13824)
        O_R3 = take(R3W)
        O_WB = take(8192)
        O_CONST = take(2784)
        assert off <= NWORDS, off

        hT = sview(O_R1, [128, 8, T], BF16)
        hT_b = [Buf("hT%d" % i) for i in range(NT)]
        hT_b2 = [Buf("hTd%d" % i) for i in range(NT)]
        brT = [sview(O_R2 + 4608 * j, [128, 4, T], BF16) for j in range(3)]
        br_b = [[Buf("br%d_%d" % (j, i)) for i in range(NT)] for j in range(3)]
        WB = [sview(O_WB + 2048 * i, [128, 8, 512], BF16) for i in range(4)]
        WB_b = [Buf("wb%d" % i) for i in range(4)]
        wb_rr = [0]

        def wb_next():
            i = wb_rr[0]
            wb_rr[0] = (i + 1) % 4
            return WB[i], WB_b[i]

        co = O_CONST
        ident = sview(co, [128, 128], F32); co += 128
        rmat = sview(co, [128, 128], F32); co += 128
        ecT = sview(co, [128, 64], F32); co += 64
        scT = sview(co, [128, 8, 2], F32); co += 16
        modT_l = []; VT_l = []; gT_l = []; A0_l = []; A1_l = []; G2_l = []; G5_l = []
        for _l in range(DEPTH):
            modT_l.append(sview(co, [128, 48, 2], F32)); co += 96
            VT_l.append(sview(co, [128, NV], F32)); co += NV
            gT_l.append(VT_l[_l][:, 0:32].rearrange("p (g c) -> p g c", g=4))
            A0_l.append(sview(co, [128, 8, 2], F32)); co += 16
            A1_l.append(sview(co, [128, 8, 2], F32)); co += 16
            G2_l.append(sview(co, [128, 8, 2], F32)); co += 16
            G5_l.append(sview(co, [128, 8, 2], F32)); co += 16
        scB = sview(co, [128, 8, 2], BF16); co += 8
        c8T = sview(co, [128, 2, 2, 4], F32); co += 16
        hbT = sview(co, [128, 2, 2, 4], F32); co += 16
        NSTAT = 8
        stat = sview(co, [128, NSTAT, 4], F32); co += 4 * NSTAT
        PM = sview(co, [128, 4, 128], BF16); co += 256
        BD = sview(co, [128, 16, 128], BF16); co += 1024
        junkb = sview(co, [128, 1024], BF16); co += 512
        assert co <= O_CONST + 2784, co - O_CONST
        stat_b = [Buf("stat%d" % i) for i in range(NSTAT)]
        stat_rr = [0]
        mod_b = [Buf("mod%d" % i) for i in range(DEPTH)]
        moda_b = [Buf("moda%d" % i) for i in range(DEPTH)]
        AGa_b = [Buf("AGa%d" % i) for i in range(DEPTH)]
        gT_b = [Buf("gT%d" % i) for i in range(DEPTH)]
        AG_b = [Buf("AG%d" % i) for i in range(DEPTH)]
        cb = {n: Buf(n) for n in ["ident", "rmat", "mask", "ones2", "ec", "sc", "esk", "cw", "scB",
                                  "lb", "c8", "ps", "stat", "PM", "BD", "junk"]}

        bank = [PS[i // 2][:, (i % 2) * 512:(i % 2) * 512 + 512] for i in range(8)]
        bank_b = [Buf("bank%d" % i, excl=True) for i in range(8)]
        pair = [PS[i][:, :] for i in range(4)]

        def mm(out_ap, lhsT, rhs, start, stop, reads, writes):
            k.op("pe", lambda e: e.matmul(out_ap, lhsT=lhsT, rhs=rhs, start=start, stop=stop), reads=reads, writes=writes)

        def dump(name, ap, bufs, dt=F32):
            if name not in dumps:
                return
            shp = list(ap.shape)
            t_ = nc.dram_tensor("dbg_" + name, shp, dt, kind="ExternalOutput").ap()
            b_ = Buf("dbg_" + name)
            k.dma("sp", lambda e: e.dma_start(out=t_, in_=ap), reads=bufs, writes=[b_])
            dump_out[name] = b_

        k.dma("sp", lambda e: e.dma_start(out=ident, in_=dram["k_ident"]), writes=[cb["ident"]])
        k.dma("sp", lambda e: e.dma_start(out=rmat, in_=dram["k_rmat"]), writes=[cb["rmat"]])
        k.dma("sp", lambda e: e.dma_start(out=ecT, in_=dram["k_ec"]), writes=[cb["ec"]])
        k.dma("sp", lambda e: e.dma_start(out=scT, in_=dram["cvecT"]), writes=[cb["sc"]])
        for _l in range(DEPTH):
            k.dma("sp", lambda e, _l=_l: e.dma_start(out=VT_l[_l], in_=dram["vecs"][_l]), writes=[gT_b[_l]])
        k.op("act", lambda e: e.activation(out=scT, in_=scT, func=AF.Silu), reads=[cb["sc"]], writes=[cb["sc"]])
        k.op("act", lambda e: e.activation(out=scB, in_=scT, func=AF.Copy), reads=[cb["sc"]], writes=[cb["scB"]])

        tile_src_x = lambda i: dram["ctx"][i * 128:(i + 1) * 128, :] if i < 2 else dram["x"][(i - 2) * 128:(i - 1) * 128, :]
        xin_b = [Buf("xin%d" % i) for i in range(NT)]
        xa_b = [Buf("xa%d" % i) for i in range(NT)]
        xb_b = [Buf("xb%d" % i) for i in range(NT)]
        y_b = [Buf("y%d" % i) for i in range(NT)]

        r3_prev = []

        def r3_fence():
            f = tokens_of(r3_prev)
            r3_prev.clear()
            return f

        def r3buf(name, fence):
            b_ = Buf(name, fence)
            r3_prev.append(b_)
            return b_

        def next_stat():
            i = stat_rr[0]
            stat_rr[0] = (i + 1) % NSTAT
            return stat[:, i, :], stat_b[i]

        def rstd_of(src_ap, src_bufs):
            st_, st_b = next_stat()
            k.op("act", lambda e: e.activation(out=junkb, in_=src_ap, func=AF.Square, accum_out=st_[:, 0:1]), reads=src_bufs, writes=[cb["junk"], st_b])
            k.op("act", lambda e: e.activation(out=st_[:, 1:2], in_=st_[:, 0:1], func=AF.Sqrt, scale=1.0 / D, bias=EPS), reads=[st_b], writes=[st_b])
            k.op("dve", lambda e: e.reciprocal(out=st_[:, 2:3], in_=st_[:, 1:2]), reads=[st_b], writes=[st_b])
            return st_[:, 2:3], st_b

        def norm_stats(xt, xt_b, xs, xs_b):
            rs, rs_b = rstd_of(xt, [xt_b])
            k.op("dve", lambda e: e.tensor_scalar(out=xs, in0=xt, scalar1=rs, scalar2=None, op0=ALU.mult), reads=[xt_b, rs_b], writes=[xs_b])

        def norm_transpose(xs, xs_b, Acoef, Bcoef, coef_bufs, r, tcol, pbank):
            for half in range(2):
                pb = pbank[half]
                for kk in range(4):
                    kc = half * 4 + kk
                    k.op("pe", lambda e, kc=kc, kk=kk, pb=pb: e.transpose(bank[pb][:, kk * 128:(kk + 1) * 128], xs[:, kc * 128:(kc + 1) * 128], ident),
                         reads=[xs_b, cb["ident"]], writes=[bank_b[pb]])
                for kk in range(4):
                    kc = half * 4 + kk
                    if half == 0:
                        k.op("act", lambda e, kc=kc, kk=kk, pb=pb: e.activation(out=hT[:, kc, tcol:tcol + 128], in_=bank[pb][:, kk * 128:(kk + 1) * 128],
                                                                              func=AF.Identity, scale=Acoef[:, kc, r:r + 1], bias=Bcoef[:, kc, r:r + 1]),
                             reads=[bank_b[pb]] + coef_bufs, writes=[hT_b[tcol // 128]])
                    else:
                        k.op("dve", lambda e, kc=kc, kk=kk, pb=pb: e.tensor_scalar(out=hT[:, kc, tcol:tcol + 128], in0=bank[pb][:, kk * 128:(kk + 1) * 128],
                                                                                 scalar1=Acoef[:, kc, r:r + 1], scalar2=Bcoef[:, kc, r:r + 1], op0=ALU.mult, op1=ALU.add),
                             reads=[bank_b[pb]] + coef_bufs, writes=[hT_b2[tcol // 128]])

        mod_state = {}

        def mod_setup(o_base, fence):
            wa = [sview(o_base + 2048 * i, [128, 8, 512], BF16) for i in range(4)]
            mrow = [sview(o_base + 8192 + 512 * i, [128, 512], F32) for i in range(2)]
            mod_state.update(wa=wa, wa_b=[r3buf("wa%d" % i, fence) for i in range(4)], mrow=mrow, mrow_b=[r3buf("mrow%d" % i, fence) for i in range(2)], n=0)

        def mod_chunk(l, n_):
            ms = mod_state
            q = ms["n"]; ms["n"] += 1
            wa, wa_b, mrow, mrow_b = ms["wa"][q % 4], ms["wa_b"][q % 4], ms["mrow"][q % 2], ms["mrow_b"][q % 2]
            k.dma("pool", lambda e: e.dma_start(out=wa, in_=dram["w_ada"][l][:, n_ * 512:(n_ + 1) * 512].rearrange("(k p) n -> p k n", p=128)), writes=[wa_b])
            k.dma("sp", lambda e: e.dma_start(out=mrow[0:2, :], in_=dram["b_ada"][l:l + 1, n_ * 512:(n_ + 1) * 512].partition_broadcast(2)), writes=[mrow_b])
            for kc in range(8):
                mm(bank[6][0:2, :], scB[:, kc, :], wa[:, kc, :], kc == 0, kc == 7, [wa_b, cb["scB"]], [bank_b[6]])
            k.op("dve", lambda e: e.tensor_tensor(out=mrow[0:2, :], in0=bank[6][0:2, :], in1=mrow[0:2, :], op=ALU.add), reads=[bank_b[6], mrow_b], writes=[mrow_b])
            for cc in range(4):
                ch = l * 48 + n_ * 4 + cc
                k.op("pe", lambda e, cc=cc, ch=ch: e.transpose(bank[7][:, ch * 2:ch * 2 + 2], mrow[0:2, cc * 128:(cc + 1) * 128], ident[0:2, 0:2]),
                     reads=[mrow_b, cb["ident"]], writes=[bank_b[7]])

        def mod_finish_a(l):
            modT, gT, A0 = modT_l[l], gT_l[l], A0_l[l]
            k.op("dve", lambda e: e.tensor_copy(out=modT[:, 0:16, :], in_=bank[7][:, l * 96:l * 96 + 32].rearrange("p (c r) -> p c r", r=2)), reads=[bank_b[7]], writes=[moda_b[l]])
            for r in range(2):
                k.op("dve", lambda e, r=r: e.scalar_tensor_tensor(out=A0[:, :, r], in0=modT[:, 8:16, r], scalar=1.0, in1=gT[:, 0, :], op0=ALU.add, op1=ALU.mult), reads=[moda_b[l], gT_b[l]], writes=[AGa_b[l]])

        def mod_finish_b(l):
            modT, gT, A1, G2, G5 = modT_l[l], gT_l[l], A1_l[l], G2_l[l], G5_l[l]
            k.op("dve", lambda e: e.tensor_copy(out=modT[:, 16:48, :], in_=bank[7][:, l * 96 + 32:l * 96 + 96].rearrange("p (c r) -> p c r", r=2)), reads=[bank_b[7]], writes=[mod_b[l]])
            for r in range(2):
                k.op("dve", lambda e, r=r: e.scalar_tensor_tensor(out=A1[:, :, r], in0=modT[:, 32:40, r], scalar=1.0, in1=gT[:, 2, :], op0=ALU.add, op1=ALU.mult), reads=[mod_b[l], gT_b[l]], writes=[AG_b[l]])
                k.op("dve", lambda e, r=r: e.tensor_tensor(out=G2[:, :, r], in0=modT[:, 16:24, r], in1=gT[:, 1, :], op=ALU.mult), reads=[mod_b[l], gT_b[l]], writes=[AG_b[l]])
                k.op("dve", lambda e, r=r: e.tensor_tensor(out=G5[:, :, r], in0=modT[:, 40:48, r], in1=gT[:, 3, :], op=ALU.mult), reads=[mod_b[l], gT_b[l]], writes=[AG_b[l]])

        for l in range(DEPTH):
            need_ctx = l < DEPTH - 1
            tiles = list(range(NT)) if need_ctx else list(range(2, NT))
            blk_all = blocks_of(0, T)
            blk_out = blk_all if need_ctx else blocks_of(C_CTX, T)
            src_b = xin_b if l == 0 else xb_b
            src_ap = (lambda i: tile_src_x(i)) if l == 0 else (lambda i: xb[i * 128:(i + 1) * 128, :])
            w_in = dram["w_in"][l]

            VT = VT_l[l]
            cwT = VT[:, 32:48].rearrange("p (c t) -> p c t", c=4)
            cbT = VT[:, 48:52]
            lbT = VT[:, 52:76].rearrange("p (v d c) -> p v d c", v=3, d=2)
            psT = VT[:, 76:80]
            esk = VT[:, 80:84]
            for nm_ in ("esk", "cw", "lb", "ps"):
                cb[nm_] = Buf(nm_ + str(l))
                cb[nm_].w = gT_b[l].w
            k.op("act", lambda e: e.activation(out=esk, in_=esk, func=AF.Exp), reads=[cb["esk"]], writes=[cb["esk"]])
            k.op("act", lambda e: e.activation(out=c8T[:, 0, :, :], in_=lbT[:, 2, :, :], func=AF.Sigmoid), reads=[cb["lb"]], writes=[cb["c8"]])
            k.op("act", lambda e: e.activation(out=c8T[:, 0, :, :], in_=c8T[:, 0, :, :], func=AF.Ln), reads=[cb["c8"]], writes=[cb["c8"]])
            k.op("dve", lambda e: e.tensor_scalar(out=c8T[:, 1, :, :], in0=c8T[:, 0, :, :], scalar1=8.0, scalar2=None, op0=ALU.mult), reads=[cb["c8"]], writes=[cb["c8"]])
            k.op("dve", lambda e: e.tensor_scalar(out=c8T[:, 0, :, :], in0=c8T[:, 0, :, :], scalar1=4.0, scalar2=None, op0=ALU.mult), reads=[cb["c8"]], writes=[cb["c8"]])
            k.op("dve", lambda e: e.tensor_scalar(out=hbT, in0=lbT[:, 0:2, :, :], scalar1=0.5, scalar2=None, op0=ALU.mult), reads=[cb["lb"]], writes=[cb["c8"]])
            fence = r3_fence()
            NXT, NXS = 4, 3
            xt = [sview(O_R3 + 1024 * i, [128, 1024], F32) for i in range(NXT)]
            xt_b = [r3buf("xt%d" % i, fence) for i in range(NXT)]
            xs = [sview(O_R3 + 1024 * NXT + 1024 * i, [128, 1024], F32) for i in range(NXS)]
            xs_b = [r3buf("xs%d" % i, fence) for i in range(NXS)]
            o_mod = O_R3 + 1024 * (NXT + NXS)
            assert o_mod + 8192 + 1024 - O_R3 <= R3W
            pending = []
            if l == 0:
                mod_setup(o_mod, fence)
                for n_ in range(4):
                    mod_chunk(0, n_)
                mod_finish_a(0)
                pending = [(0, n_) for n_ in range(4, 12)] + [(l_, n_) for l_ in range(1, DEPTH) for n_ in range(12)]
            modT, A0, A1, G2, G5 = modT_l[l], A0_l[l], A1_l[l], G2_l[l], G5_l[l]
            B0 = modT[:, 0:8, :]
            B1 = modT[:, 24:32, :]
            coef_a = [AGa_b[l], moda_b[l]]
            coef_b = [AG_b[l], mod_b[l]]
            if stop == "mod":
                break
            done_l0 = [False]

            def ride_along(nmax):
                for _ in range(nmax):
                    if not pending:
                        return
                    l_, n_ = pending.pop(0)
                    mod_chunk(l_, n_)
                    if l_ == 0 and n_ == 11:
                        mod_finish_b(0)
            jobs = []
            for i in range(NT):
                def st0(i=i):
                    s_ = i % NXT
                    k.dma("sp", lambda e: e.dma_start(out=xt[s_], in_=src_ap(i)), reads=[src_b[i]], writes=[xt_b[s_]])
                    norm_stats(xt[s_], xt_b[s_], xs[i % NXS], xs_b[i % NXS])
                    ride_along(2 if i < 2 else 1)

                def st1(i=i):
                    norm_transpose(xs[i % NXS], xs_b[i % NXS], A0, B0, coef_a, 1 if i < 2 else 0, i * 128, (2 * (i % 2), 2 * (i % 2) + 1))
                jobs.append((st0, st1))
            run_pipe(jobs, 2)
            if l == 0:
                ride_along(99)
                for l_ in range(1, DEPTH):
                    mod_finish_a(l_)
                    mod_finish_b(l_)
                dump("modT", modT, [mod_b[0], moda_b[0]])
            if l == 0:
                dump("hT", hT, hT_b + hT_b2, BF16)
            if stop == "PA":
                break

            fence = r3_fence()
            o3 = O_R3
            QT = sview(o3, [128, 4, T], BF16); o3 += 4608
            KT = sview(o3, [128, 2, T], BF16); o3 += 2304
            V2 = sview(o3, [128, NT, 2, 192], BF16); o3 += 3456
            o_rope = o3
            rope = sview(o3, [128, 2, S_LAT], F32); o3 += 4096
            qraw = [sview(o3 + 512 * i, [128, 512], F32) for i in range(3)]; o3 += 1536
            t1 = [sview(o3 + 512 * i, [128, 512], F32) for i in range(2)]; o3 += 1024
            assert o3 - O_R3 <= R3W, o3 - O_R3
            QT_b = [r3buf("QT%d" % i, fence) for i in range(NT)]
            KT_b = [r3buf("KT%d" % i, fence) for i in range(NT)]
            V2_b = [r3buf("V2_%d" % i, fence) for i in range(NT)]
            rope_b = r3buf("rope", fence)
            qraw_b = [r3buf("qraw%d" % i, fence) for i in range(3)]
            t1_b = [r3buf("t1_%d" % i, fence) for i in range(2)]
            k.dma("sp", lambda e: e.dma_start(out=rope, in_=dram["k_rope"]), writes=[rope_b])
            wq, wq_b = wb_next()
            k.dma("pool", lambda e: e.dma_start(out=wq, in_=w_in[:, Q0:Q0 + 512].rearrange("(k p) n -> p k n", p=128)), writes=[wq_b])
            wkv, wkv_b = wb_next()
            for h in range(2):
                for dup in range(2):
                    k.dma("pool", lambda e, h=h, dup=dup: e.dma_start(out=wkv[:, :, h * 128 + dup * 64:h * 128 + dup * 64 + 64],
                                                                    in_=w_in[:, K0 + h * 64:K0 + (h + 1) * 64].rearrange("(k p) n -> p k n", p=128)), writes=[wkv_b])
            k.dma("pool", lambda e: e.dma_start(out=wkv[:, :, 256:384], in_=w_in[:, V0:V0 + 128].rearrange("(k p) n -> p k n", p=128)), writes=[wkv_b])
            k.op("dve", lambda e: e.memset(V2, 1.0), writes=V2_b)
            rr = 0
            jobs = []
            jn = 0
            for which, nch in (("q", 4), ("k", 2)):
                for c_ in range(nch):
                    blks = blk_out if which == "q" else blk_all
                    for (t0, n) in blks:
                        wt_, wtb_ = (wq, wq_b) if which == "q" else (wkv, wkv_b)
                        dst, dst_b = (QT, QT_b) if which == "q" else (KT, KT_b)
                        tl = list(range(t0 // 128, (t0 + n) // 128))
                        pb = jn % 3
                        qi = jn % 3
                        jn += 1

                        def st0(c_=c_, t0=t0, n=n, wt_=wt_, wtb_=wtb_, dst=dst, dst_b=dst_b, tl=tl, pb=pb, qi=qi):
                            for kc in range(8):
                                mm(bank[pb][:, 0:n], wt_[:, kc, c_ * 128:(c_ + 1) * 128], hT[:, kc, t0:t0 + n], kc == 0, kc == 7,
                                   [wtb_] + [hT_b[i] for i in tl] + [hT_b2[i] for i in tl], [bank_b[pb]])
                            if t0 < C_CTX:
                                k.op("act", lambda e: e.activation(out=dst[:, c_, t0:t0 + n], in_=bank[pb][:, 0:n], func=AF.Copy),
                                     reads=[bank_b[pb]], writes=[dst_b[i] for i in tl])
                            else:
                                k.op("act", lambda e: e.activation(out=qraw[qi][:, 0:n], in_=bank[pb][:, 0:n], func=AF.Copy),
                                     reads=[bank_b[pb]], writes=[qraw_b[qi]])

                        def st1(c_=c_, t0=t0, n=n, dst=dst, dst_b=dst_b, tl=tl, pb=pb, qi=qi):
                            if t0 < C_CTX:
                                return
                            s0 = t0 - C_CTX
                            pb2 = 3 + pb
                            ti = qi % 2
                            mm(bank[pb2][:, 0:n], rmat, qraw[qi][:, 0:n], True, True, [cb["rmat"], qraw_b[qi]], [bank_b[pb2]])
                            k.op("pool", lambda e: e.tensor_tensor(out=t1[ti][:, 0:n], in0=qraw[qi][:, 0:n], in1=rope[:, 0, s0:s0 + n], op=ALU.mult),
                                 reads=[qraw_b[qi], rope_b], writes=[t1_b[ti]])
                            k.op("dve", lambda e: e.tensor_tensor(out=qraw[qi][:, 0:n], in0=bank[pb2][:, 0:n], in1=rope[:, 1, s0:s0 + n], op=ALU.mult),
                                 reads=[bank_b[pb2], rope_b], writes=[qraw_b[qi]])
                            k.op("dve", lambda e: e.tensor_tensor(out=dst[:, c_, t0:t0 + n], in0=t1[ti][:, 0:n], in1=qraw[qi][:, 0:n], op=ALU.add),
                                 reads=[t1_b[ti], qraw_b[qi]], writes=[dst_b[i] for i in tl])
                        jobs.append((st0, st1))
            run_pipe(jobs, 1)
            for i in range(NT):
                pb = 6 + i % 2
                for kc in range(8):
                    mm(bank[pb][:, 0:128], hT[:, kc, i * 128:(i + 1) * 128], wkv[:, kc, 256:384], kc == 0, kc == 7, [wkv_b, hT_b[i], hT_b2[i]], [bank_b[pb]])
                for o_ in (0, 128):
                    k.op("act", lambda e, i=i, pb=pb, o_=o_: e.activation(out=V2[:, i, :, o_:o_ + 64], in_=bank[pb][:, 0:128].rearrange("p (h d) -> p h d", h=2), func=AF.Copy),
                         reads=[bank_b[pb]], writes=[V2_b[i]])
            if l == 0:
                dump("QT", QT, QT_b, BF16)
                dump("KT", KT, KT_b, BF16)
                dump("V2", V2, V2_b, BF16)
            if stop == "B1":
                break

            fence2 = tokens_of([rope_b] + qraw_b + t1_b)
            o2 = o_rope
            NPT, NET = 5, 4
            ptp = [sview(o2 + 512 * i, [128, 2, 512], BF16) for i in range(NPT)]; o2 += 512 * NPT
            etp = [sview(o2 + 512 * i, [128, 2, 512], BF16) for i in range(NET)]; o2 += 512 * NET
            dtot = [sview(o2 + 512 * i, [128, 512], F32) for i in range(2)]; o2 += 1024
            mask2 = sview(o2, [128, 2, 384], BF16); o2 += 384
            assert o2 - O_R3 <= R3W, o2 - O_R3
            ptp_b = [r3buf("ptp%d" % i, fence2) for i in range(NPT)]
            etp_b = [r3buf("etp%d" % i, fence2) for i in range(NET)]
            dtot_b = [r3buf("dtot%d" % i, fence2) for i in range(2)]
            mask2_b = r3buf("mask2", fence2)
            k.dma("pool", lambda e: e.dma_start(out=mask2, in_=dram["k_mask2"]), writes=[mask2_b])
            attnT = brT[0]
            Spair = [PS[i][:, :].rearrange("p (b n) -> p b n", b=2) for i in range(2)]
            sups = [("lat", s_) for s_ in range(4)] + ([("ctx", 0)] if need_ctx else [])
            jobs = []
            cnt_ = {"s": 0, "p": 0, "e": 0, "it": 0, "m": 0}
            for c_ in range(4):
                h = c_ // 2
                for kind, s_ in sups:
                    it = cnt_["it"]; cnt_["it"] += 1
                    Eb = 4 + 2 * (it % 2); Ob = Eb + 1
                    if kind == "lat":
                        q0 = C_CTX + 512 * s_; nq = 512
                    else:
                        q0 = 0; nq = 256
                    qtl = list(range(q0 // 128, (q0 + nq) // 128))
                    kbs = [(kb * 128, kb, 0, nq, None) for kb in range(2)]
                    if kind == "lat":
                        for j in range(4 * s_ - 1, 4 * s_ + 5):
                            if j < 0 or j > 15:
                                continue
                            n_lo = max(j - 1, 4 * s_); n_hi = min(j + 1, 4 * s_ + 3)
                            kbs.append((C_CTX + 128 * j, 2 + j, (n_lo - 4 * s_) * 128, (n_hi - n_lo + 1) * 128, (n_lo - (j - 1)) * 128))
                    for ji, (kcol, vt, qo, qn, mo) in enumerate(kbs):
                        sp = cnt_["s"] % 2; cnt_["s"] += 1
                        pi = cnt_["p"] % NPT; cnt_["p"] += 1
                        first = ji == 0
                        last = ji == len(kbs) - 1

                        def stage0(c_=c_, h=h, kcol=kcol, q0=q0, qo=qo, qn=qn, mo=mo, sp=sp, pi=pi):
                            qb = [QT_b[i] for i in range((q0 + qo) // 128, (q0 + qo + qn) // 128)]
                            for e_ in range(2):
                                ps_ = slice(e_ * 64, (e_ + 1) * 64)
                                mm(Spair[sp][:, e_, 0:qn], KT[ps_, h, kcol:kcol + 128], QT[ps_, c_, q0 + qo:q0 + qo + qn], True, True,
                                   [KT_b[kcol // 128]] + qb, [bank_b[2 * sp + e_]])
                            sb_ = [bank_b[2 * sp], bank_b[2 * sp + 1]]
                            if mo is None:
                                k.op("act", lambda e: e.activation(out=ptp[pi][:, :, 0:qn], in_=Spair[sp][:, :, 0:qn], func=AF.Exp, scale=0.125),
                                     reads=sb_, writes=[ptp_b[pi]])
                            else:
                                ei = cnt_["e"] % NET; cnt_["e"] += 1
                                k.op("act", lambda e: e.activation(out=etp[ei][:, :, 0:qn], in_=Spair[sp][:, :, 0:qn], func=AF.Exp, scale=0.125),
                                     reads=sb_, writes=[etp_b[ei]])
                                meng = "dve" if cnt_["m"] % 4 == 0 else "pool"; cnt_["m"] += 1
                                k.op(meng, lambda e: e.tensor_tensor(out=ptp[pi][:, :, 0:qn], in0=etp[ei][:, :, 0:qn], in1=mask2[:, :, mo:mo + qn], op=ALU.mult),
                                     reads=[etp_b[ei], mask2_b], writes=[ptp_b[pi]])

                        def stage1(c_=c_, h=h, vt=vt, qo=qo, qn=qn, pi=pi, first=first, last=last, Eb=Eb, Ob=Ob, nq=nq, q0=q0, qtl=qtl, it=it):
                            mm(bank[Eb][:, qo:qo + qn], V2[:, vt, h, 0:128], ptp[pi][:, 0, 0:qn], first, last, [V2_b[vt], ptp_b[pi]], [bank_b[Eb]])
                            mm(bank[Ob][:, qo:qo + qn], V2[:, vt, h, 64:192], ptp[pi][:, 1, 0:qn], first, last, [V2_b[vt], ptp_b[pi]], [bank_b[Ob]])
                            if last:
                                di = it % 2
                                lo_, hi_ = slice(0, 64), slice(64, 128)
                                k.op("dve", lambda e: e.tensor_scalar(out=dtot[di][lo_, 0:nq], in0=bank[Eb][hi_, 0:nq], scalar1=esk[hi_, c_:c_ + 1], scalar2=None, op0=ALU.add),
                                     reads=[bank_b[Eb], cb["esk"]], writes=[dtot_b[di]])
                                k.op("dve", lambda e: e.tensor_scalar(out=dtot[di][hi_, 0:nq], in0=bank[Ob][lo_, 0:nq], scalar1=esk[lo_, c_:c_ + 1], scalar2=None, op0=ALU.add),
                                     reads=[bank_b[Ob], cb["esk"]], writes=[dtot_b[di]])
                                k.op("dve", lambda e: e.reciprocal(out=dtot[di][:, 0:nq], in_=dtot[di][:, 0:nq]), reads=[dtot_b[di]], writes=[dtot_b[di]])
                                k.op("dve", lambda e: e.tensor_tensor(out=attnT[lo_, c_, q0:q0 + nq], in0=bank[Eb][lo_, 0:nq], in1=dtot[di][lo_, 0:nq], op=ALU.mult),
                                     reads=[bank_b[Eb], dtot_b[di]], writes=[br_b[0][i] for i in qtl])
                                k.op("dve", lambda e: e.tensor_tensor(out=attnT[hi_, c_, q0:q0 + nq], in0=bank[Ob][hi_, 0:nq], in1=dtot[di][hi_, 0:nq], op=ALU.mult),
                                     reads=[bank_b[Ob], dtot_b[di]], writes=[br_b[0][i] for i in qtl])
                        jobs.append((stage0, stage1))
            run_pipe(jobs, 3)
            if l == 0:
                dump("attnT", attnT, br_b[0], BF16)
            if stop == "B2":
                break

            fence = r3_fence()
            o3 = O_R3
            NP = 2312
            H0 = 1280
            Abuf = sview(o3, [128, NP], F32); o3 += NP
            XCs = [sview(o3 + T * i, [128, T], F32) for i in range(2)]; o3 += 2 * T
            XCB = sview(o3, [128, T], BF16); o3 += T // 2
            Rs = [sview(o3 + H0 * i, [128, H0], F32) for i in range(2)]; o3 += 2 * H0
            Is = [sview(o3 + H0 * i, [128, H0], F32) for i in range(2)]; o3 += 2 * H0
            Ms = [sview(o3 + H0 * i, [128, H0], F32) for i in range(2)]; o3 += 2 * H0
            HS = sview(o3, [128, T], F32); o3 += T
            gy = [sview(o3 + 512 * i, [128, 512], F32) for i in range(2)]; o3 += 1024
            assert o3 - O_R3 <= R3W, o3 - O_R3
            A_b = r3buf("A", fence); XC_bs = [r3buf("XC%d" % i, fence) for i in range(2)]; XCB_b = r3buf("XCB", fence)
            R_bs = [r3buf("R%d" % i, fence) for i in range(2)]; I_bs = [r3buf("I%d" % i, fence) for i in range(2)]; M_bs = [r3buf("M%d" % i, fence) for i in range(2)]
            HS_bs = [r3buf("HS%d" % i, fence) for i in range(NT)]
            gy_b = [r3buf("gy%d" % i, fence) for i in range(2)]
            wrx, wrx_b = wb_next()
            k.dma("pool", lambda e: e.dma_start(out=wrx, in_=w_in[:, RX0:RX0 + 512].rearrange("(k p) n -> p k n", p=128)), writes=[wrx_b])
            wry, wry_b = wb_next()
            k.dma("pool", lambda e: e.dma_start(out=wry, in_=w_in[:, RY0:RY0 + 512].rearrange("(k p) n -> p k n", p=128)), writes=[wry_b])
            k.dma("pool", lambda e: e.dma_start(out=PM, in_=dram["pool_mix"][l].rearrange("g c d -> c g d")), writes=[cb["PM"]])
            k.op("dve", lambda e: e.memset(BD, 0.0), writes=[cb["BD"]])
            for d_ in range(2):
                for gi, gn in enumerate(["lru_w_a", "lru_w_x"]):
                    base = (d_ * 2 + gi) * 4
                    for e_ in range(2):
                        o_ = e_ * 64
                        k.dma("pool", lambda e, d_=d_, gn=gn, base=base, e_=e_, o_=o_: e.dma_start(out=BD[o_:o_ + 64, base:base + 4, o_:o_ + 64],
                                                                                                 in_=dram[gn][l, d_, e_::2].rearrange("h i j -> i h j")), writes=[cb["BD"]])
            rnnT = brT[2]
            poff = lambda t: t + 1 if t < C_CTX else t + 4
            k.op("dve", lambda e: e.memset(Abuf, 0.0), writes=[A_b])
            rrc = {"a": 0, "g": 0, "y": 0}
            hsb = lambda c0, c1: [HS_bs[i] for i in range(c0 // 128, (c1 + 127) // 128)]

            def stPa(cc):
                for (t0, n) in blk_all:
                    pb = rrc["a"] % 2; rrc["a"] += 1
                    tl = list(range(t0 // 128, (t0 + n) // 128))
                    for kc in range(8):
                        mm(bank[pb][:, 0:n], wrx[:, kc, cc * 128:(cc + 1) * 128], hT[:, kc, t0:t0 + n], kc == 0, kc == 7, [wrx_b] + [hT_b[i] for i in tl] + [hT_b2[i] for i in tl], [bank_b[pb]])
                    k.op("act", lambda e: e.activation(out=Abuf[:, poff(t0):poff(t0) + n], in_=bank[pb][:, 0:n], func=AF.Copy), reads=[bank_b[pb]], writes=[A_b])

            def stPb(cc):
                XC = XCs[cc % 2]; XC_b = XC_bs[cc % 2]
                for (o_, ln, oc) in ((1, C_CTX, 0), (260, S_LAT, C_CTX)):
                    k.op("dve", lambda e: e.tensor_scalar(out=XC[:, oc:oc + ln], in0=Abuf[:, o_ - 1:o_ - 1 + ln], scalar1=cwT[:, cc, 0:1], scalar2=cbT[:, cc:cc + 1], op0=ALU.mult, op1=ALU.add),
                         reads=[A_b, cb["cw"]], writes=[XC_b])
                    for tap in range(1, 4):
                        k.op("dve", lambda e, tap=tap: e.scalar_tensor_tensor(out=XC[:, oc:oc + ln], in0=Abuf[:, o_ - 1 + tap:o_ - 1 + tap + ln], scalar=cwT[:, cc, tap:tap + 1], in1=XC[:, oc:oc + ln], op0=ALU.mult, op1=ALU.add),
                             reads=[A_b, XC_b, cb["cw"]], writes=[XC_b])

            def gate_jobs(cc):
                XC = XCs[cc % 2]; XC_b = XC_bs[cc % 2]
                jl = []
                s2 = (cc + 2) % 2
                for ji, (d_, c0, c1, sl) in enumerate([(0, 0, H0, cc % 2), (0, H0, T, (cc + 1) % 2), (1, 0, H0, cc % 2), (1, H0, T, (cc + 1) % 2)]):
                    R, I_, M = Rs[sl], Is[sl], Ms[sl]
                    R_b, I_b, M_b = R_bs[sl], I_bs[sl], M_bs[sl]
                    w_ = c1 - c0
                    blks = [(t0, n) for (t0, n) in blk_all if c0 <= t0 < c1]

                    def s0(d_=d_, c0=c0, c1=c1, w_=w_, blks=blks, R=R, I_=I_, M=M, R_b=R_b, I_b=I_b, M_b=M_b, ji=ji):
                        if ji == 0:
                            k.op("pool", lambda e: e.tensor_copy(out=XCB, in_=XC), reads=[XC_b], writes=[XCB_b])
                        for gi in range(2):
                            dstg, dstg_b = (R, R_b) if gi == 0 else (I_, I_b)
                            idx = (d_ * 2 + gi) * 4 + cc
                            for (t0, n) in blks:
                                pb = 2 + rrc["g"] % 2; rrc["g"] += 1
                                mm(bank[pb][:, 0:n], BD[:, idx, :], XCB[:, t0:t0 + n], True, True, [cb["BD"], XCB_b], [bank_b[pb]])
                                k.op("act", lambda e, pb=pb, n=n, t0=t0, dstg=dstg, gi=gi: e.activation(out=dstg[:, t0 - c0:t0 - c0 + n], in_=bank[pb][:, 0:n], func=AF.Tanh, scale=0.5, bias=hbT[:, gi, d_, cc:cc + 1]),
                                     reads=[bank_b[pb], cb["c8"]], writes=[dstg_b])
                        k.op("act", lambda e: e.activation(out=M[:, 0:w_], in_=R[:, 0:w_], func=AF.Exp, scale=c8T[:, 1, d_, cc:cc + 1], bias=c8T[:, 1, d_, cc:cc + 1]), reads=[R_b, cb["c8"]], writes=[M_b])
                        k.op("act", lambda e: e.activation(out=R[:, 0:w_], in_=R[:, 0:w_], func=AF.Exp, scale=c8T[:, 0, d_, cc:cc + 1], bias=c8T[:, 0, d_, cc:cc + 1]), reads=[R_b, cb["c8"]], writes=[R_b])
                        k.op("act", lambda e: e.activation(out=M[:, 0:w_], in_=M[:, 0:w_], func=AF.Sqrt, scale=-1.0, bias=1.0), reads=[M_b], writes=[M_b])

                    def s1(d_=d_, c0=c0, c1=c1, w_=w_, R=R, I_=I_, M=M, R_b=R_b, I_b=I_b, M_b=M_b, ji=ji):
                        k.op("dve", lambda e: e.scalar_tensor_tensor(out=I_[:, 0:w_], in0=I_[:, 0:w_], scalar=1.0, in1=XC[:, c0:c1], op0=ALU.add, op1=ALU.mult), reads=[I_b, XC_b], writes=[I_b])
                        k.op("dve", lambda e: e.tensor_tensor(out=I_[:, 0:w_], in0=I_[:, 0:w_], in1=M[:, 0:w_], op=ALU.mult), reads=[I_b, M_b], writes=[I_b])
                        if d_ == 0:
                            init = 0.0 if c0 == 0 else HS[:, c0 - 1:c0]
                            rd = [] if c0 == 0 else [HS_bs[(c0 - 1) // 128]]
                            k.op("dve", lambda e: e.tensor_tensor_scan(out=HS[:, c0:c1], data0=R[:, 0:w_], data1=I_[:, 0:w_], initial=init, op0=ALU.mult, op1=ALU.add),
                                 reads=[R_b, I_b] + rd, writes=hsb(c0, c1))
                        elif c0 == 0:
                            k.op("dve", lambda e: e.tensor_tensor_scan(out=M[:, 0:C_CTX][:, ::-1], data0=R[:, 0:C_CTX][:, ::-1], data1=I_[:, 0:C_CTX][:, ::-1], initial=0.0, op0=ALU.mult, op1=ALU.add),
                                 reads=[R_b, I_b, M_b], writes=[M_b])
                        else:
                            R0, I0, M0 = Rs[s2], Is[s2], Ms[s2]
                            k.op("dve", lambda e: e.tensor_tensor_scan(out=M[:, 0:w_][:, ::-1], data0=R[:, 0:w_][:, ::-1], data1=I_[:, 0:w_][:, ::-1], initial=M0[:, 0:1], op0=ALU.mult, op1=ALU.add),
                                 reads=[R_b, I_b, M_b, M_bs[s2]], writes=[M_b])
                            k.op("dve", lambda e: e.tensor_tensor(out=HS[:, H0:T], in0=HS[:, H0:T], in1=M[:, 0:w_], op=ALU.add), reads=hsb(H0, T) + [M_b], writes=hsb(H0, T))
                            k.op("dve", lambda e: e.tensor_tensor_scan(out=M0[:, C_CTX:H0][:, ::-1], data0=R0[:, C_CTX:H0][:, ::-1], data1=I0[:, C_CTX:H0][:, ::-1], initial=M[:, 0:1], op0=ALU.mult, op1=ALU.add),
                                 reads=[R_bs[s2], I_bs[s2], M_bs[s2], M_b], writes=[M_bs[s2]])
                            k.op("dve", lambda e: e.tensor_tensor(out=HS[:, 0:H0], in0=HS[:, 0:H0], in1=M0[:, 0:H0], op=ALU.add), reads=hsb(0, H0) + [M_bs[s2]], writes=hsb(0, H0))
                    jl.append((s0, s1))
                return jl

            def stY(cc):
                for (t0, n) in blk_out:
                    pb = 4 + rrc["y"] % 2; rrc["y"] += 1
                    gi_ = pb - 4
                    tl = list(range(t0 // 128, (t0 + n) // 128))
                    for kc in range(8):
                        mm(bank[pb][:, 0:n], wry[:, kc, cc * 128:(cc + 1) * 128], hT[:, kc, t0:t0 + n], kc == 0, kc == 7, [wry_b] + [hT_b[i] for i in tl] + [hT_b2[i] for i in tl], [bank_b[pb]])
                    k.op("act", lambda e: e.activation(out=gy[gi_][:, 0:n], in_=bank[pb][:, 0:n], func=AF.Gelu_apprx_tanh), reads=[bank_b[pb]], writes=[gy_b[gi_]])
                    k.op("dve", lambda e: e.scalar_tensor_tensor(out=rnnT[:, cc, t0:t0 + n], in0=HS[:, t0:t0 + n], scalar=0.5, in1=gy[gi_][:, 0:n], op0=ALU.mult, op1=ALU.mult),
                         reads=[HS_bs[i] for i in tl] + [gy_b[gi_]], writes=[br_b[2][i] for i in tl])

            stPa(0)
            stPb(0)
            for cc in range(4):
                jl = gate_jobs(cc)
                nxt = cc + 1 < 4
                jl[0][0]()
                if nxt:
                    stPa(cc + 1)
                jl[1][0]()
                jl[0][1]()
                jl[2][0]()
                jl[1][1]()
                if nxt:
                    stPb(cc + 1)
                jl[3][0]()
                jl[2][1]()
                jl[3][1]()
                stY(cc)
            if l == 0:
                dump("rnnT", rnnT, br_b[2], BF16)
            if stop == "B3":
                break

            fence = r3_fence()
            o3 = O_R3
            NPP = 2336
            PUs = [sview(o3 + NPP * i, [128, NPP], F32) for i in range(2)]; o3 += 2 * NPP
            SAs = [sview(o3 + NPP * i, [128, NPP], F32) for i in range(2)]; o3 += 2 * NPP
            SBs = [sview(o3 + NPP * i, [128, NPP], F32) for i in range(2)]; o3 += 2 * NPP
            Dbs = [sview(o3 + (NPP // 2) * i, [128, NPP], BF16) for i in range(2)]; o3 += NPP
            etmps = [sview(o3 + 32 * i, [128, 32], F32) for i in range(2)]; o3 += 64
            assert o3 - O_R3 <= R3W, o3 - O_R3
            PU_bs = [r3buf("PU%d" % i, fence) for i in range(2)]; SA_bs = [r3buf("SA%d" % i, fence) for i in range(2)]
            SB_bs = [r3buf("SB%d" % i, fence) for i in range(2)]
            D_bs = [[r3buf("D%d_%d" % (i, j), fence) for j in range(5)] for i in range(2)]
            et2_bs = [[r3buf("etmp%d_%d" % (i, j), fence) for j in range(4)] for i in range(2)]
            wpu, wpu_b = wb_next()
            k.dma("pool", lambda e: e.dma_start(out=wpu, in_=w_in[:, PU0:PU0 + 512].rearrange("(k p) n -> p k n", p=128)), writes=[wpu_b])
            poolT = brT[1]
            qoff = lambda t: t + 8 if t < C_CTX else t + 24
            for i_ in range(2):
                k.op("dve", lambda e, i_=i_: e.memset(PUs[i_], 0.0), writes=[PU_bs[i_]])
            rrp = {"a": 0, "m": 0}
            jobs = []
            for g in range(4):
                w = 2 << g
                pa_ = g % 2
                PU, PU_b, SA, SA_b, SBf, SB_b, Dbf, D_b, etmp, et2_b = PUs[pa_], PU_bs[pa_], SAs[pa_], SA_bs[pa_], SBs[pa_], SB_bs[pa_], Dbs[pa_], D_bs[pa_], etmps[pa_], et2_bs[pa_]

                def stP(g=g, PU=PU, PU_b=PU_b):
                    for (t0, n) in blk_out:
                        pb = rrp["a"] % 2; rrp["a"] += 1
                        tl = list(range(t0 // 128, (t0 + n) // 128))
                        for kc in range(8):
                            mm(bank[pb][:, 0:n], wpu[:, kc, g * 128:(g + 1) * 128], hT[:, kc, t0:t0 + n], kc == 0, kc == 7, [wpu_b] + [hT_b[i] for i in tl] + [hT_b2[i] for i in tl], [bank_b[pb]])
                        k.op("act", lambda e: e.activation(out=PU[:, qoff(t0):qoff(t0) + n], in_=bank[pb][:, 0:n], func=AF.Copy), reads=[bank_b[pb]], writes=[PU_b])

                def stD(g=g, w=w, PU=PU, PU_b=PU_b, SA=SA, SA_b=SA_b, SBf=SBf, SB_b=SB_b, Dbf=Dbf, D_b=D_b, etmp=etmp, et2_b=et2_b):
                    eng = "pool" if g % 2 == 0 else "dve"
                    k.op(eng, lambda e: e.tensor_tensor(out=SA[:, 0:NPP - 1], in0=PU[:, 0:NPP - 1], in1=PU[:, 1:NPP], op=ALU.add), reads=[PU_b], writes=[SA_b])
                    cur, cur_b, oth, oth_b = SA, SA_b, SBf, SB_b
                    lo, hi = 0, NPP - 1
                    sh = 1
                    for step in range(g):
                        nlo, nhi = lo + sh, hi - sh
                        k.op(eng, lambda e, cur=cur, oth=oth, nlo=nlo, nhi=nhi, sh=sh: e.tensor_tensor(out=oth[:, nlo:nhi], in0=cur[:, nlo - sh:nhi - sh], in1=cur[:, nlo + sh:nhi + sh], op=ALU.add),
                             reads=[cur_b], writes=[oth_b])
                        cur, cur_b, oth, oth_b = oth, oth_b, cur, cur_b
                        lo, hi = nlo, nhi
                        sh *= 2
                    k.op("dve", lambda e: e.scalar_tensor_tensor(out=Dbf[:, lo:hi], in0=cur[:, lo:hi], scalar=1.0 / w, in1=PU[:, lo:hi], op0=ALU.mult, op1=ALU.subtract),
                         reads=[cur_b, PU_b], writes=D_b)
                    segs = ((8, C_CTX), (280, S_LAT)) if need_ctx else ((280, S_LAT),)
                    edges = [(o_ if side == 0 else o_ + ln - 8, side) for (o_, ln) in segs for side in range(2)]
                    for ei_, (c0, side) in enumerate(edges):
                        k.op(eng, lambda e, c0=c0, side=side, ei_=ei_: e.tensor_tensor(out=etmp[:, ei_ * 8:(ei_ + 1) * 8], in0=cur[:, c0:c0 + 8], in1=ecT[:, (g * 2 + side) * 8:(g * 2 + side) * 8 + 8], op=ALU.mult),
                             reads=[cur_b, cb["ec"]], writes=[et2_b[ei_]])
                    for ei_, (c0, side) in enumerate(edges):
                        k.op(eng, lambda e, c0=c0, ei_=ei_: e.tensor_tensor(out=Dbf[:, c0:c0 + 8], in0=etmp[:, ei_ * 8:(ei_ + 1) * 8], in1=PU[:, c0:c0 + 8], op=ALU.subtract),
                             reads=[et2_b[ei_], PU_b], writes=[D_b[1 + ei_]])

                def stM(g=g, Dbf=Dbf, D_b=D_b):
                    for (t0, n) in blk_out:
                        pb = 2 + rrp["m"] % 2; rrp["m"] += 1
                        tl = list(range(t0 // 128, (t0 + n) // 128))
                        mm(bank[pb][:, 0:n], PM[:, g, :], Dbf[:, qoff(t0):qoff(t0) + n], True, True, [cb["PM"]] + D_b, [bank_b[pb]])
                        k.op("act", lambda e: e.activation(out=poolT[:, g, t0:t0 + n], in_=bank[pb][:, 0:n], func=AF.Copy, scale=psT[:, g:g + 1]),
                             reads=[bank_b[pb], cb["ps"]], writes=[br_b[1][i] for i in tl])
                jobs.append((stP, stD, stM))
            run_pipe(jobs, 1)
            if l == 0:
                dump("poolT", poolT, br_b[1], BF16)
            if stop == "B4":
                break

            fence = r3_fence()
            o3 = O_R3
            MG = sview(o3, [128, 8, T], BF16); o3 += 9216
            WG = [sview(O_WB + 4096 * i, [128, 8, 3, 128], BF16) for i in range(2)]
            WJ = [sview(O_WB + 4096 * i + 1536, [128, 3, 4, 128], BF16) for i in range(2)]
            gs = [sview(o3 + 512 * i, [128, 512], F32) for i in range(2)]; o3 += 1024
            acc = [sview(o3 + 512 * i, [128, 512], F32) for i in range(2)]; o3 += 1024
            tt_ = [sview(o3 + 512 * i, [128, 512], F32) for i in range(2)]; o3 += 1024
            assert o3 - O_R3 <= R3W, o3 - O_R3
            MG_b = [r3buf("MG%d" % i, fence) for i in range(NT)]
            WGJ_b = [[WB_b[2 * i], WB_b[2 * i + 1]] for i in range(2)]
            gs_b = [r3buf("gs%d" % i, fence) for i in range(2)]
            acc_b = [r3buf("acc%d" % i, fence) for i in range(2)]
            tt_b = [r3buf("tt%d" % i, fence) for i in range(2)]
            wjo = [dram["w_attn_o"][l], dram["w_pool_o"][l], dram["w_rnn_o"][l]]

            def load_m(m):
                s_ = m % 2
                for j in range(3):
                    k.dma("pool", lambda e, j=j, m=m, s_=s_: e.dma_start(out=WG[s_][:, :, j, :], in_=w_in[:, GL0 + j * 1024 + m * 128:GL0 + j * 1024 + (m + 1) * 128].rearrange("(k p) n -> p k n", p=128)), writes=WGJ_b[s_])
                    k.dma("pool", lambda e, j=j, m=m, s_=s_: e.dma_start(out=WJ[s_][:, j, :, :], in_=wjo[j][:, m * 128:(m + 1) * 128].rearrange("(k p) n -> p k n", p=128)), writes=WGJ_b[s_])
            load_m(0)
            gi_ = 0
            ai = 0
            for m in range(8):
                if m + 1 < 8:
                    load_m(m + 1)
                s_ = m % 2
                for (t0, n) in blk_out:
                    tl = list(range(t0 // 128, (t0 + n) // 128))
                    a_ = ai % 2; ai += 1
                    for j in range(3):
                        pg = (rr % 2) * 2; py = pg + 1; rr += 1
                        for kc in range(8):
                            mm(bank[pg][:, 0:n], WG[s_][:, kc, j, :], hT[:, kc, t0:t0 + n], kc == 0, kc == 7, WGJ_b[s_] + [hT_b[i] for i in tl] + [hT_b2[i] for i in tl], [bank_b[pg]])
                        for kc in range(4):
                            mm(bank[py][:, 0:n], WJ[s_][:, j, kc, :], brT[j][:, kc, t0:t0 + n], kc == 0, kc == 3, WGJ_b[s_] + [br_b[j][i] for i in tl], [bank_b[py]])
                        g_ = gi_ % 2; gi_ += 1
                        k.op("act", lambda e, pg=pg, n=n, g_=g_: e.activation(out=gs[g_][:, 0:n], in_=bank[pg][:, 0:n], func=AF.Sigmoid), reads=[bank_b[pg]], writes=[gs_b[g_]])
                        if j == 0:
                            k.op("dve", lambda e, py=py, n=n, g_=g_, a_=a_: e.tensor_tensor(out=acc[a_][:, 0:n], in0=bank[py][:, 0:n], in1=gs[g_][:, 0:n], op=ALU.mult),
                                 reads=[bank_b[py], gs_b[g_]], writes=[acc_b[a_]])
                        else:
                            k.op("dve", lambda e, py=py, n=n, g_=g_: e.tensor_tensor(out=tt_[g_][:, 0:n], in0=bank[py][:, 0:n], in1=gs[g_][:, 0:n], op=ALU.mult),
                                 reads=[bank_b[py], gs_b[g_]], writes=[tt_b[g_]])
                            if j == 1:
                                k.op("dve", lambda e, n=n, g_=g_, a_=a_: e.tensor_tensor(out=acc[a_][:, 0:n], in0=acc[a_][:, 0:n], in1=tt_[g_][:, 0:n], op=ALU.add),
                                     reads=[acc_b[a_], tt_b[g_]], writes=[acc_b[a_]])
                            else:
                                k.op("dve", lambda e, n=n, g_=g_, a_=a_, m=m, t0=t0: e.tensor_tensor(out=MG[:, m, t0:t0 + n], in0=acc[a_][:, 0:n], in1=tt_[g_][:, 0:n], op=ALU.add),
                                     reads=[acc_b[a_], tt_b[g_]], writes=[MG_b[i] for i in tl])
            if l == 0:
                dump("MG", MG, MG_b, BF16)
            if stop == "B5":
                break

            o3 = O_R3 + 9216
            NX6, NS6 = 3, 3
            GB = [sview(o3 + 1024 * i, [128, 1024], F32) for i in range(2)]; o3 += 2048
            xt6 = [sview(o3 + 1024 * i, [128, 1024], F32) for i in range(NX6)]; o3 += 1024 * NX6
            xs6 = [sview(o3 + 1024 * i, [128, 1024], F32) for i in range(NS6)]; o3 += 1024 * NS6
            dg = sview(o3, [128, 128], F32); o3 += 128
            onesf = sview(o3, [128, 128], F32); o3 += 128
            assert o3 - O_R3 <= R3W, o3 - O_R3
            fence6 = tokens_of(gs_b + acc_b + tt_b)
            wb_rr[0] = 0
            GB_b = [Buf("GB%d" % i, fence6) for i in range(2)]
            xt6_b = [Buf("xt6_%d" % i, fence6) for i in range(NX6)]
            xs6_b = [Buf("xs6_%d" % i, fence6) for i in range(NS6)]
            dg_b = Buf("dg", fence6)
            onesf_b = Buf("onesf", fence6)
            r3_prev.extend(GB_b + xt6_b + xs6_b + [dg_b, onesf_b])

            def build_GB(Gc):
                for r in range(2 if need_ctx else 1):
                    for kc in range(8):
                        k.op("dve", lambda e, kc=kc, r=r: e.tensor_scalar(out=dg, in0=ident, scalar1=Gc[:, kc, r:r + 1], scalar2=None, op0=ALU.mult),
                             reads=[cb["ident"], AG_b[l]], writes=[dg_b])
                        pb = 4 + kc // 4
                        k.op("pe", lambda e, kc=kc, pb=pb: e.matmul(bank[pb][:, (kc % 4) * 128:(kc % 4) * 128 + 128], lhsT=onesf, rhs=dg, start=True, stop=True),
                             reads=[dg_b, onesf_b], writes=[bank_b[pb]])
                    for half in range(2):
                        k.op("act", lambda e, half=half, r=r: e.activation(out=GB[r][:, half * 512:(half + 1) * 512], in_=bank[4 + half], func=AF.Copy),
                             reads=[bank_b[4 + half]], writes=[GB_b[r]])
            k.op("dve", lambda e: e.memset(onesf, 1.0), writes=[onesf_b])
            build_GB(G2)
            wo = [None, None]; wo_b = [None, None]
            for half in range(2):
                wo[half], wo_b[half] = wb_next()
                k.dma("pool", lambda e, half=half: e.dma_start(out=wo[half], in_=dram["w_out"][l][:, half * 512:(half + 1) * 512].rearrange("(k p) n -> p k n", p=128)), writes=[wo_b[half]])

            def post_update(ps_pair, ps_bufs, xt_ap, xt_buf, tmp_ap, tmp_buf, r):
                rs, rs_b = rstd_of(ps_pair, ps_bufs)
                k.op("dve", lambda e: e.scalar_tensor_tensor(out=tmp_ap, in0=ps_pair, scalar=rs, in1=GB[r], op0=ALU.mult, op1=ALU.mult),
                     reads=ps_bufs + [rs_b, GB_b[r]], writes=[tmp_buf])
                k.op("pool", lambda e: e.tensor_tensor(out=xt_ap, in0=xt_ap, in1=tmp_ap, op=ALU.add), reads=[tmp_buf, xt_buf], writes=[xt_buf])

            jobs = []
            for ii, i in enumerate(tiles):
                def st0(ii=ii, i=i):
                    s_ = ii % NX6
                    pp = ii % 3
                    k.dma("sp", lambda e: e.dma_start(out=xt6[s_], in_=src_ap(i)), reads=[src_b[i]], writes=[xt6_b[s_]])
                    for half in range(2):
                        for kc in range(8):
                            mm(pair[pp][:, half * 512:(half + 1) * 512], MG[:, kc, i * 128:(i + 1) * 128], wo[half][:, kc, :], kc == 0, kc == 7, [MG_b[i], wo_b[half]], [bank_b[2 * pp + half]])

                def st1(ii=ii, i=i):
                    s_ = ii % NX6
                    pp = ii % 3
                    x2 = ii % NS6
                    r = 1 if i < 2 else 0
                    post_update(pair[pp], [bank_b[2 * pp], bank_b[2 * pp + 1]], xt6[s_], xt6_b[s_], xs6[x2], xs6_b[x2], r)
                    k.dma("sp", lambda e: e.dma_start(out=xa[i * 128:(i + 1) * 128, :], in_=xt6[s_]), reads=[xt6_b[s_]], writes=[xa_b[i]])

                def st2(ii=ii, i=i):
                    s_ = ii % NX6
                    x2 = ii % NS6
                    norm_stats(xt6[s_], xt6_b[s_], xs6[x2], xs6_b[x2])

                def st3(ii=ii, i=i):
                    x2 = ii % NS6
                    r = 1 if i < 2 else 0
                    norm_transpose(xs6[x2], xs6_b[x2], A1, B1, coef_b, r, i * 128, (6, 7))
                jobs.append((st0, st1, st2, st3))
            run_pipe(jobs, 1)
            if l == 0:
                dump("h2T", hT, hT_b + hT_b2, BF16)
            if stop == "B6":
                break

            fence = r3_fence()
            fenceR2 = tokens_of([b_ for row in br_b for b_ in row])
            o3 = O_R3
            WD = sview(o3, [128, NM, 1024], BF16); o3 += 11264
            GBf = [sview(o3 + 1024 * i, [128, 1024], F32) for i in range(2)]; o3 += 2048
            xt7 = [sview(o3 + 1024 * i, [128, 1024], F32) for i in range(2)]; o3 += 2048
            sg = [sview(o3 + 512 * i, [128, 512], F32) for i in range(2)]; o3 += 1024
            dg = sview(o3, [128, 128], F32); o3 += 128
            onesf = sview(o3, [128, 128], F32); o3 += 128
            assert o3 - O_R3 <= R3W, o3 - O_R3
            WD_b = r3buf("WD", fence)
            GB = GBf
            GB_b = [r3buf("GBf%d" % i, fence) for i in range(2)]
            xt7_b = [r3buf("xt7_%d" % i, fence) for i in range(2)]
            sg_b = [r3buf("sg%d" % i, fence) for i in range(2)]
            dg_b = r3buf("dgf", fence)
            onesf_b = r3buf("onesff", fence)
            k.op("dve", lambda e: e.memset(onesf, 1.0), writes=[onesf_b])
            build_GB(G5)
            lo_t = 0 if need_ctx else C_CTX
            half_t = (T - lo_t) // 2
            groups = [(lo_t, lo_t + half_t), (lo_t + half_t, T)]
            GW = half_t
            actT = sview(O_R2, [128, NM, GW], BF16)
            act_b = [Buf("act%d" % i, fenceR2) for i in range(GW // 128)]
            for b_ in [b2 for row in br_b for b2 in row]:
                b_.w = None; b_.r = {}
            w_gu = dram["w_gu"][l]
            dst_ap = (lambda i: xb[i * 128:(i + 1) * 128, :]) if need_ctx else (lambda i: y[(i - 2) * 128:(i - 1) * 128, :])
            dst_b = xb_b if need_ctx else y_b
            def load_wd():
                for q4 in range(4):
                    m0 = q4 * 6
                    m1 = min(NM, m0 + 6)
                    k.dma("pool", lambda e, m0=m0, m1=m1: e.dma_start(out=WD[:, m0:m1, :], in_=dram["w_down"][l][m0 * 128:m1 * 128, :].rearrange("(m p) n -> p m n", p=128)), writes=[WD_b])
            for gi__, (g0, g1) in enumerate(groups):
                gblks = blocks_of(g0, g1)
                for mq in range(6):
                    if gi__ == 0 and mq == 2:
                        load_wd()
                    m0 = mq * 4
                    nm_ = min(4, NM - m0)
                    wg_, wg_b = wb_next()
                    k.dma("pool", lambda e, m0=m0, nm_=nm_, wg_=wg_: e.dma_start(out=wg_[:, :, 0:nm_ * 128], in_=w_gu[:, m0 * 128:(m0 + nm_) * 128].rearrange("(k p) n -> p k n", p=128)), writes=[wg_b])
                    wu_, wu_b = wb_next()
                    k.dma("pool", lambda e, m0=m0, nm_=nm_, wu_=wu_: e.dma_start(out=wu_[:, :, 0:nm_ * 128], in_=w_gu[:, D_FF + m0 * 128:D_FF + (m0 + nm_) * 128].rearrange("(k p) n -> p k n", p=128)), writes=[wu_b])
                    for mi in range(nm_):
                        m = m0 + mi
                        for (t0, n) in gblks:
                            tl = list(range(t0 // 128, (t0 + n) // 128))
                            pg = (rr % 2) * 2; pu_ = pg + 1; rr += 1
                            for kc in range(8):
                                mm(bank[pg][:, 0:n], wg_[:, kc, mi * 128:(mi + 1) * 128], hT[:, kc, t0:t0 + n], kc == 0, kc == 7, [wg_b] + [hT_b[i] for i in tl] + [hT_b2[i] for i in tl], [bank_b[pg]])
                            for kc in range(8):
                                mm(bank[pu_][:, 0:n], wu_[:, kc, mi * 128:(mi + 1) * 128], hT[:, kc, t0:t0 + n], kc == 0, kc == 7, [wu_b] + [hT_b[i] for i in tl] + [hT_b2[i] for i in tl], [bank_b[pu_]])
                            g_ = gi_ % 2; gi_ += 1
                            k.op("act", lambda e, pg=pg, n=n, g_=g_: e.activation(out=sg[g_][:, 0:n], in_=bank[pg][:, 0:n], func=AF.Silu), reads=[bank_b[pg]], writes=[sg_b[g_]])
                            k.op("dve", lambda e, pu_=pu_, n=n, g_=g_, m=m, t0=t0, g0=g0: e.tensor_tensor(out=actT[:, m, t0 - g0:t0 - g0 + n], in0=bank[pu_][:, 0:n], in1=sg[g_][:, 0:n], op=ALU.mult),
                                 reads=[bank_b[pu_], sg_b[g_]], writes=[act_b[(i * 128 - g0) // 128] for i in tl])
                for ii, i in enumerate(range(g0 // 128, g1 // 128)):
                    s_ = ii % 2
                    r = 1 if i < 2 else 0
                    k.dma("sp", lambda e, i=i, s_=s_: e.dma_start(out=xt7[s_], in_=xa[i * 128:(i + 1) * 128, :]), reads=[xa_b[i]], writes=[xt7_b[s_]])
                    pp = 2 + s_
                    ia = i - g0 // 128
                    for half in range(2):
                        for m in range(NM):
                            mm(pair[pp][:, half * 512:(half + 1) * 512], actT[:, m, ia * 128:(ia + 1) * 128], WD[:, m, half * 512:(half + 1) * 512], m == 0, m == NM - 1,
                               [act_b[ia], WD_b], [bank_b[2 * pp + half]])
                    tmpf = sview(O_R3 + 11264 + 2048 + 2048, [128, 1024], F32)
                    rs, rs_b = rstd_of(pair[pp], [bank_b[2 * pp], bank_b[2 * pp + 1]])
                    k.op("dve", lambda e, pp=pp, r=r, tmpf=tmpf, rs=rs: e.scalar_tensor_tensor(out=tmpf, in0=pair[pp], scalar=rs, in1=GB[r], op0=ALU.mult, op1=ALU.mult),
                         reads=[bank_b[2 * pp], bank_b[2 * pp + 1], rs_b, GB_b[r]], writes=[sg_b[0], sg_b[1]])
                    k.op("pool", lambda e, s_=s_, tmpf=tmpf: e.tensor_tensor(out=xt7[s_], in0=xt7[s_], in1=tmpf, op=ALU.add), reads=[xt7_b[s_], sg_b[0], sg_b[1]], writes=[xt7_b[s_]])
                    k.dma("sp", lambda e, i=i, s_=s_: e.dma_start(out=dst_ap(i), in_=xt7[s_]), reads=[xt7_b[s_]], writes=[dst_b[i]])
            r3_prev.extend(act_b)
            act_tok = tokens_of(act_b)
            for row in br_b:
                for b_ in row:
                    b_.w = None
                    b_.r = dict(act_tok)
            if stop == "L%d" % l:
                break

        outs = list(dump_out.values())
        if stop is None:
            outs += y_b[2:]
        k.wait_all("sp", outs)
        k.emit(st)
    return nc


def kernel(**inputs):
    consts = make_consts()
    vecs = make_vecs(inputs)
    nc = build()
    B = inputs["x"].shape[0]
    in_maps = []
    for b in range(B):
        m = {"x": np.ascontiguousarray(inputs["x"][b], dtype=np.float32),
             "ctx": np.ascontiguousarray(inputs["ctx"][b], dtype=np.float32),
             "cvecT": np.ascontiguousarray(np.stack([inputs["c"][b], inputs["c_ctx"]], 0).astype(np.float32).reshape(2, 8, 128).transpose(2, 1, 0)),
             "vecs": vecs}
        for n in WEIGHT_NAMES:
            m[n] = np.ascontiguousarray(inputs[n], dtype=np.float32)
        for n in CONST_SHAPES:
            m["k_" + n] = consts[n]
        in_maps.append(m)
    res = run_bass_kernel_spmd(nc, in_maps, core_ids=list(range(B)))
    return np.stack([np.asarray(r["y"]) for r in res.results], 0).astype(np.float32)
```
